# Optimizing a Trainium2 kernel written in Bass

```python
import math
import jax
import jax.numpy as jnp
from jax import lax
import numpy as np

D_MODEL = 1024
BATCH = 8
SEQ = 2048
DEPTH = 2

MEM_LEN = 256
HEAD_DIM = 64
MIX_WIDTH = 512
ROPE_THETA = 10000.0
NORM_EPS = 1e-6
NEG_INF = -1e30

SSD_HEADS = 8
SSD_INNER = SSD_HEADS * HEAD_DIM
SSD_GROUPS = 2
SSD_STATE = 64
SSD_CONV = 4
SSD_CONV_DIM = SSD_INNER + 2 * SSD_GROUPS * SSD_STATE
SSD_CHUNK = 128

RET_HEADS = 8
RET_DIM = RET_HEADS * HEAD_DIM
RET_CHUNK = 128

MOBA_HEADS = 8
MOBA_DIM = MOBA_HEADS * HEAD_DIM
MOBA_BLOCK = 256
MOBA_TOPK = 3
MOBA_QCHUNK = 32

DIL_HEADS = 8
DIL_DIM = DIL_HEADS * HEAD_DIM
DIL_PATTERNS = ((128, 1), (512, 4), (2048, 16))

N_BRANCHES = 4

X_HEADS = 4
X_HEAD_DIM = D_MODEL // X_HEADS

PEER_HEADS = 8
PEER_NKEYS = 128
PEER_EXPERTS = PEER_NKEYS * PEER_NKEYS
PEER_QDIM = 256
PEER_TOPK = 16
PEER_TCHUNK = 128

IN_SIZES = (SSD_INNER, SSD_CONV_DIM, SSD_HEADS,
            RET_DIM, RET_DIM, RET_DIM, RET_DIM,
            MOBA_DIM, MOBA_DIM, MOBA_DIM,
            DIL_DIM, DIL_DIM, DIL_DIM,
            N_BRANCHES * D_MODEL)
IN_DIM = sum(IN_SIZES)

kernel_name = 'hybrid_gated_ssd_ret_moba_dilated_peer'


def _split_points():
    pts, acc = [], 0
    for size in IN_SIZES[:-1]:
        acc += size
        pts.append(acc)
    return pts


def rms_norm(x, gain):
    xf = x.astype(jnp.float32)
    y = xf * lax.rsqrt(jnp.mean(xf * xf, axis=-1, keepdims=True) + NORM_EPS)
    return (y * gain.astype(jnp.float32)).astype(x.dtype)


def rope(t):
    seq, dh = t.shape[1], t.shape[-1]
    half = dh // 2
    inv_freq = ROPE_THETA ** (-jnp.arange(half, dtype=jnp.float32) / half)
    ang = jnp.arange(seq, dtype=jnp.float32)[:, None] * inv_freq[None, :]
    cos = jnp.cos(ang)[None, :, None, :]
    sin = jnp.sin(ang)[None, :, None, :]
    tf = t.astype(jnp.float32)
    t1, t2 = tf[..., :half], tf[..., half:]
    return jnp.concatenate([t1 * cos - t2 * sin, t2 * cos + t1 * sin], axis=-1).astype(t.dtype)


def softmax_stats(s):
    m = jnp.max(s, axis=-1, keepdims=True)
    e = jnp.exp(s - m)
    l = jnp.sum(e, axis=-1, keepdims=True)
    return e / l, (m + jnp.log(l))[..., 0]


def causal_dwconv(x, w, b):
    k, c = w.shape
    xp = jnp.pad(x, ((0, 0), (k - 1, 0), (0, 0)))
    y = lax.conv_general_dilated(xp, w[:, None, :].astype(x.dtype), window_strides=(1,), padding='VALID',
                                 dimension_numbers=('NWC', 'WIO', 'NWC'), feature_group_count=c)
    return y + b.astype(x.dtype)


def ssd_mixer(z, xbc, dt_raw, conv_w, conv_b, dt_bias, a_log, d_skip, norm_gain):
    bsz, seq, _ = z.shape
    nh, p, n, lc = SSD_HEADS, HEAD_DIM, SSD_STATE, SSD_CHUNK
    nc = seq // lc
    xbc = jax.nn.silu(causal_dwconv(xbc, conv_w, conv_b))
    xs, bm, cm = jnp.split(xbc, [SSD_INNER, SSD_INNER + SSD_GROUPS * n], axis=-1)
    rep = nh // SSD_GROUPS
    xs = xs.reshape(bsz, seq, nh, p)
    bm = jnp.repeat(bm.reshape(bsz, seq, SSD_GROUPS, n), rep, axis=2)
    cm = jnp.repeat(cm.reshape(bsz, seq, SSD_GROUPS, n), rep, axis=2)
    dt = jax.nn.softplus(dt_raw.astype(jnp.float32) + dt_bias.astype(jnp.float32))
    log_a = dt * -jnp.exp(a_log.astype(jnp.float32))
    xdt = (xs * dt[..., None]).reshape(bsz, nc, lc, nh, p)
    bc = bm.reshape(bsz, nc, lc, nh, n)
    cc = cm.reshape(bsz, nc, lc, nh, n)
    acum = jnp.cumsum(log_a.reshape(bsz, nc, lc, nh), axis=2)
    tri = jnp.tril(jnp.ones((lc, lc), dtype=bool))[None, None, :, :, None]
    seg = acum[:, :, :, None, :] - acum[:, :, None, :, :]
    decay = jnp.exp(jnp.where(tri, seg, -jnp.inf))
    scores = jnp.einsum('bclhn,bcshn->bclsh', cc, bc) * decay
    y_diag = jnp.einsum('bclsh,bcshp->bclhp', scores, xdt)
    to_end = jnp.exp(acum[:, :, -1:, :] - acum)
    chunk_states = jnp.einsum('bclhn,bclh,bclhp->bchpn', bc, to_end, xdt)
    chunk_decay = jnp.exp(acum[:, :, -1, :])

    def step(state, inp):
        st, dec = inp
        return state * dec[:, :, None, None] + st, state

    init = jnp.zeros((bsz, nh, p, n), chunk_states.dtype)
    _, prev = lax.scan(step, init, (jnp.moveaxis(chunk_states, 1, 0), jnp.moveaxis(chunk_decay, 1, 0)))
    prev = jnp.moveaxis(prev, 0, 1)
    y_off = jnp.einsum('bclhn,bchpn,bclh->bclhp', cc, prev, jnp.exp(acum))
    y = (y_diag + y_off).reshape(bsz, seq, nh, p) + xs * d_skip.astype(jnp.float32)[:, None]
    y = y.reshape(bsz, seq, SSD_INNER)
    return rms_norm(y * jax.nn.silu(z.astype(jnp.float32)), norm_gain).astype(z.dtype)


def retention_mixer(q, k, v, g, norm_gain):
    bsz, seq, _ = q.shape
    nh, dh, lc = RET_HEADS, HEAD_DIM, RET_CHUNK
    nc = seq // lc
    q = rope(q.reshape(bsz, seq, nh, dh))
    k = rope(k.reshape(bsz, seq, nh, dh)) * dh ** -0.5
    v = v.reshape(bsz, seq, nh, dh)
    log_gamma = jnp.log1p(-jnp.exp2(-5.0 - jnp.arange(nh, dtype=jnp.float32)))
    idx = jnp.arange(lc, dtype=jnp.float32)
    rel = idx[:, None] - idx[None, :]
    dmat = jnp.where((rel >= 0)[..., None], jnp.exp(jnp.maximum(rel, 0.0)[..., None] * log_gamma), 0.0)
    qc = q.reshape(bsz, nc, lc, nh, dh)
    kc = k.reshape(bsz, nc, lc, nh, dh)
    vc = v.reshape(bsz, nc, lc, nh, dh)
    inner = jnp.einsum('bclhd,bcshd->bclsh', qc, kc) * dmat
    y_inner = jnp.einsum('bclsh,bcshe->bclhe', inner, vc)
    zeta = jnp.exp((lc - 1.0 - idx)[:, None] * log_gamma)
    chunk_states = jnp.einsum('bcshd,sh,bcshe->bchde', kc, zeta, vc)
    chunk_decay = jnp.exp(lc * log_gamma)

    def step(state, st):
        return state * chunk_decay[None, :, None, None] + st, state

    init = jnp.zeros((bsz, nh, dh, dh), chunk_states.dtype)
    _, prev = lax.scan(step, init, jnp.moveaxis(chunk_states, 1, 0))
    prev = jnp.moveaxis(prev, 0, 1)
    xi = jnp.exp((idx + 1.0)[:, None] * log_gamma)
    y_cross = jnp.einsum('bclhd,bchde,lh->bclhe', qc, prev, xi)
    y = (y_inner + y_cross).astype(jnp.float32).reshape(bsz, seq, nh, dh)
    mu = jnp.mean(y, axis=-1, keepdims=True)
    var = jnp.mean(jnp.square(y - mu), axis=-1, keepdims=True)
    y = ((y - mu) * lax.rsqrt(var + NORM_EPS)).reshape(bsz, seq, RET_DIM) * norm_gain.astype(jnp.float32)
    return (jax.nn.silu(g.astype(jnp.float32)) * y).astype(g.dtype)


def moba_mixer(q, k, v):
    bsz, seq, _ = q.shape
    nh, dh, blk = MOBA_HEADS, HEAD_DIM, MOBA_BLOCK
    nb = -(-seq // blk)
    sp = nb * blk
    scale = dh ** -0.5

    def heads_first(t):
        t = jnp.pad(t, ((0, 0), (0, sp - seq), (0, 0), (0, 0)))
        return t.transpose(0, 2, 1, 3)

    q = heads_first(rope(q.reshape(bsz, seq, nh, dh)))
    k = heads_first(rope(k.reshape(bsz, seq, nh, dh)))
    v = heads_first(v.reshape(bsz, seq, nh, dh))
    qb = q.reshape(bsz, nh, nb, blk, dh)
    kb = k.reshape(bsz, nh, nb, blk, dh)
    vb = v.reshape(bsz, nh, nb, blk, dh)
    tri = jnp.tril(jnp.ones((blk, blk), dtype=bool))
    s_own = jnp.einsum('bhnid,bhnjd->bhnij', qb, kb).astype(jnp.float32) * scale
    p_own, lse_own = softmax_stats(jnp.where(tri, s_own, NEG_INF))
    o_own = jnp.einsum('bhnij,bhnjd->bhnid', p_own.astype(v.dtype), vb).reshape(bsz, nh, sp, dh)
    lse_own = lse_own.reshape(bsz, nh, sp)
    n_sel = min(MOBA_TOPK, nb - 1)
    if n_sel == 0:
        out = o_own
    else:
        k_mean = jnp.mean(kb, axis=3)
        gate = jnp.einsum('bhtd,bhnd->bhtn', q, k_mean).astype(jnp.float32)
        q_block = jnp.arange(sp) // blk
        past = jnp.arange(nb)[None, :] < q_block[:, None]
        _, sel = lax.top_k(jnp.where(past, gate, -jnp.inf), n_sel)
        valid = sel < q_block[:, None]
        nq = sp // MOBA_QCHUNK

        def by_chunk(t):
            return jnp.moveaxis(t.reshape(bsz, nh, nq, MOBA_QCHUNK, *t.shape[3:]), 2, 0)

        gather_blocks = jax.vmap(jax.vmap(lambda blocks, ids: blocks[ids]))

        def attend(args):
            qc, selc, validc = args
            ksel = gather_blocks(kb, selc)
            vsel = gather_blocks(vb, selc)
            s = jnp.einsum('bhqd,bhqnjd->bhqnj', qc, ksel).astype(jnp.float32) * scale
            s = jnp.where(validc[..., None], s, NEG_INF).reshape(bsz, nh, MOBA_QCHUNK, n_sel * blk)
            p, lse = softmax_stats(s)
            o = jnp.einsum('bhqm,bhqmd->bhqd', p.astype(v.dtype),
                           vsel.reshape(bsz, nh, MOBA_QCHUNK, n_sel * blk, dh))
            return o, lse

        o_past, lse_past = lax.map(attend, (by_chunk(q), by_chunk(sel), by_chunk(valid)))
        o_past = jnp.moveaxis(o_past, 0, 2).reshape(bsz, nh, sp, dh)
        lse_past = jnp.moveaxis(lse_past, 0, 2).reshape(bsz, nh, sp)
        w = jax.nn.softmax(jnp.stack([lse_own, lse_past], axis=-1), axis=-1)
        out = w[..., :1] * o_own + w[..., 1:] * o_past
    return out[:, :, :seq].transpose(0, 2, 1, 3).reshape(bsz, seq, MOBA_DIM).astype(v.dtype)


def dilated_group(q, k, v, window, dil):
    bsz, nh, seq, dh = q.shape
    n_off = window // dil
    sd = -(-seq // dil) * dil
    ln = sd // dil
    nblk = -(-ln // n_off)
    lp = nblk * n_off

    def to_sub(t):
        t = jnp.pad(t, ((0, 0), (0, 0), (0, sd - seq), (0, 0)))
        t = t.reshape(bsz, nh, ln, dil, dh).transpose(0, 1, 3, 2, 4)
        return jnp.pad(t, ((0, 0), (0, 0), (0, 0), (0, lp - ln), (0, 0)))

    def to_band(t):
        t = jnp.pad(to_sub(t), ((0, 0), (0, 0), (0, 0), (n_off, 0), (0, 0)))
        t = t.reshape(bsz, nh, dil, nblk + 1, n_off, dh)
        return jnp.concatenate([t[:, :, :, :-1], t[:, :, :, 1:]], axis=4)

    qb = to_sub(q).reshape(bsz, nh, dil, nblk, n_off, dh)
    kb, vb = to_band(k), to_band(v)
    s = jnp.einsum('bhrnid,bhrnjd->bhrnij', qb, kb).astype(jnp.float32) * dh ** -0.5
    i = jnp.arange(n_off)[:, None]
    j = jnp.arange(2 * n_off)[None, :]
    steps = i + n_off - j
    key_pos = (jnp.arange(nblk)[:, None, None] - 1) * n_off + j[None]
    mask = (steps >= 0) & (steps <= n_off) & (key_pos >= 0)
    p, lse = softmax_stats(jnp.where(mask, s, NEG_INF))
    o = jnp.einsum('bhrnij,bhrnjd->bhrnid', p.astype(v.dtype), vb)
    o = o.reshape(bsz, nh, dil, lp, dh)[:, :, :, :ln].transpose(0, 1, 3, 2, 4).reshape(bsz, nh, sd, dh)[:, :, :seq]
    lse = lse.reshape(bsz, nh, dil, lp)[..., :ln].transpose(0, 1, 3, 2).reshape(bsz, nh, sd)[..., :seq]
    return o, lse


def dilated_mixer(q, k, v):
    bsz, seq, _ = q.shape
    q = rope(q.reshape(bsz, seq, DIL_HEADS, HEAD_DIM)).transpose(0, 2, 1, 3)
    k = rope(k.reshape(bsz, seq, DIL_HEADS, HEAD_DIM)).transpose(0, 2, 1, 3)
    v = v.reshape(bsz, seq, DIL_HEADS, HEAD_DIM).transpose(0, 2, 1, 3)
    groups = [dilated_group(q, k, v, w, d) for (w, d) in DIL_PATTERNS]
    wts = jax.nn.softmax(jnp.stack([lse for _, lse in groups], axis=-1), axis=-1)
    out = jnp.einsum('bhsg,gbhsd->bhsd', wts.astype(v.dtype), jnp.stack([o for o, _ in groups]))
    return out.transpose(0, 2, 1, 3).reshape(bsz, seq, DIL_DIM).astype(v.dtype)


def cross_attention(h, mem, w_q, w_kv, w_o):
    bsz, seq, _ = h.shape
    q = (h @ w_q).reshape(bsz, seq, X_HEADS, X_HEAD_DIM)
    k, v = jnp.split(mem @ w_kv, 2, axis=-1)
    k = k.reshape(bsz, -1, X_HEADS, X_HEAD_DIM)
    v = v.reshape(bsz, -1, X_HEADS, X_HEAD_DIM)
    s = jnp.einsum('bshd,bmhd->bhsm', q, k).astype(jnp.float32) * X_HEAD_DIM ** -0.5
    p = jax.nn.softmax(s, axis=-1).astype(v.dtype)
    o = jnp.einsum('bhsm,bmhd->bshd', p, v).reshape(bsz, seq, D_MODEL)
    return o @ w_o


def peer_ffn(h, w_q, sub_keys, expert_u, expert_v):
    bsz, seq, d = h.shape
    t = bsz * seq
    x = h.reshape(t, d)
    q = (x @ w_q).reshape(t, PEER_HEADS, 2, PEER_QDIM // 2)
    s = jnp.einsum('thpd,hpkd->thpk', q, sub_keys).astype(jnp.float32)
    s_top, i_top = lax.top_k(s, PEER_TOPK)
    cand_s = (s_top[:, :, 0, :, None] + s_top[:, :, 1, None, :]).reshape(t, PEER_HEADS, PEER_TOPK * PEER_TOPK)
    cand_i = (i_top[:, :, 0, :, None] * PEER_NKEYS + i_top[:, :, 1, None, :]).reshape(t, PEER_HEADS, PEER_TOPK * PEER_TOPK)
    best_s, pos = lax.top_k(cand_s, PEER_TOPK)
    expert_idx = jnp.take_along_axis(cand_i, pos, axis=-1)
    gate = jax.nn.softmax(best_s, axis=-1)
    nt = t // PEER_TCHUNK

    def apply(args):
        xc, ic, gc = args
        u = expert_u[ic]
        vv = expert_v[ic]
        act = jax.nn.gelu(jnp.einsum('td,thkd->thk', xc, u).astype(jnp.float32), approximate=False)
        return jnp.einsum('thk,thkd->td', (act * gc).astype(vv.dtype), vv)

    y = lax.map(apply, (x.reshape(nt, PEER_TCHUNK, d),
                        expert_idx.reshape(nt, PEER_TCHUNK, PEER_HEADS, PEER_TOPK),
                        gate.reshape(nt, PEER_TCHUNK, PEER_HEADS, PEER_TOPK)))
    return y.reshape(bsz, seq, d).astype(h.dtype)


def setup_inputs(seed: int = 0) -> dict:
    key = jax.random.key(seed)
    ks = jax.random.split(key, 23)
    f32 = jnp.float32

    def nrm(k, shape, scale):
        return jax.random.normal(k, shape, f32) * scale

    def gain(k, shape):
        return 1.0 + 0.02 * jax.random.normal(k, shape, f32)

    dt0 = jnp.exp(jax.random.uniform(ks[6], (DEPTH, SSD_HEADS), f32, math.log(1e-3), math.log(1e-1)))
    dt_bias = dt0 + jnp.log(-jnp.expm1(-dt0))
    return {
        'x': nrm(ks[0], (BATCH, SEQ, D_MODEL), 1.0),
        'mem': nrm(ks[1], (BATCH, MEM_LEN, D_MODEL), 1.0),
        'mix_norm': gain(ks[2], (DEPTH, D_MODEL)),
        'w_in': nrm(ks[3], (DEPTH, D_MODEL, IN_DIM), D_MODEL ** -0.5),
        'ssd_conv_w': nrm(ks[4], (DEPTH, SSD_CONV, SSD_CONV_DIM), SSD_CONV ** -0.5),
        'ssd_conv_b': nrm(ks[5], (DEPTH, SSD_CONV_DIM), 0.01),
        'ssd_dt_bias': dt_bias,
        'ssd_a_log': jnp.log(jax.random.uniform(ks[7], (DEPTH, SSD_HEADS), f32, 1.0, 16.0)),
        'ssd_d': 1.0 + 0.1 * jax.random.normal(ks[8], (DEPTH, SSD_HEADS), f32),
        'ssd_norm': gain(ks[9], (DEPTH, SSD_INNER)),
        'ret_norm': gain(ks[10], (DEPTH, RET_DIM)),
        'w_branch': nrm(ks[11], (DEPTH, N_BRANCHES, MIX_WIDTH, D_MODEL), MIX_WIDTH ** -0.5),
        'w_out': nrm(ks[12], (DEPTH, D_MODEL, D_MODEL), D_MODEL ** -0.5),
        'x_norm': gain(ks[13], (DEPTH, D_MODEL)),
        'w_xq': nrm(ks[14], (DEPTH, D_MODEL, D_MODEL), D_MODEL ** -0.5),
        'w_xkv': nrm(ks[15], (DEPTH, D_MODEL, 2 * D_MODEL), D_MODEL ** -0.5),
        'w_xo': nrm(ks[16], (DEPTH, D_MODEL, D_MODEL), D_MODEL ** -0.5),
        'ffn_norm': gain(ks[17], (DEPTH, D_MODEL)),
        'w_pq': nrm(ks[18], (DEPTH, D_MODEL, PEER_HEADS * PEER_QDIM), D_MODEL ** -0.5),
        'peer_sub_keys': nrm(ks[19], (DEPTH, PEER_HEADS, 2, PEER_NKEYS, PEER_QDIM // 2), (PEER_QDIM // 2) ** -0.5),
        'peer_u': nrm(ks[20], (DEPTH, PEER_EXPERTS, D_MODEL), D_MODEL ** -0.5),
        'peer_v': nrm(ks[21], (DEPTH, PEER_EXPERTS, D_MODEL), (PEER_HEADS * PEER_TOPK) ** -0.5),
        'final_norm': gain(ks[22], (D_MODEL,)),
    }


def reference(x, mem, mix_norm, w_in, ssd_conv_w, ssd_conv_b, ssd_dt_bias, ssd_a_log, ssd_d, ssd_norm,
              ret_norm, w_branch, w_out, x_norm, w_xq, w_xkv, w_xo, ffn_norm, w_pq, peer_sub_keys,
              peer_u, peer_v, final_norm):
    bsz, seq, _ = x.shape
    pts = _split_points()
    h = x
    for layer in range(DEPTH):
        hn = rms_norm(h, mix_norm[layer])
        (z, xbc, dt_raw, rq, rk, rv, rg, mq, mk, mv, dq, dk, dv,
         gate_logits) = jnp.split(hn @ w_in[layer], pts, axis=-1)
        ys = (ssd_mixer(z, xbc, dt_raw, ssd_conv_w[layer], ssd_conv_b[layer], ssd_dt_bias[layer],
                        ssd_a_log[layer], ssd_d[layer], ssd_norm[layer]),
              retention_mixer(rq, rk, rv, rg, ret_norm[layer]),
              moba_mixer(mq, mk, mv),
              dilated_mixer(dq, dk, dv))
        gates = jax.nn.sigmoid(gate_logits.reshape(bsz, seq, N_BRANCHES, D_MODEL))
        merged = sum(gates[:, :, i] * (ys[i] @ w_branch[layer, i]) for i in range(N_BRANCHES))
        h = h + merged @ w_out[layer]
        h = h + cross_attention(rms_norm(h, x_norm[layer]), mem, w_xq[layer], w_xkv[layer], w_xo[layer])
        h = h + peer_ffn(rms_norm(h, ffn_norm[layer]), w_pq[layer], peer_sub_keys[layer],
                         peer_u[layer], peer_v[layer])
    return rms_norm(h, final_norm)
```

```python
import numpy as np
from contextlib import ExitStack
import concourse.bass as bass
import concourse.mybir as mybir
from concourse.bass_utils import run_bass_kernel_spmd

F32 = mybir.dt.float32
BF16 = mybir.dt.bfloat16
U32 = mybir.dt.uint32
ALU = mybir.AluOpType
AF = mybir.ActivationFunctionType
AX = mybir.AxisListType

S = 2048
D = 1024
NT = 16
DEPTH = 2
IN_DIM = 10504
EPS = 1e-6
NEG = -1e30
C_Z, C_XBC, C_DT = 0, 512, 1280
C_RQ, C_RK, C_RV, C_RG = 1288, 1800, 2312, 2824
C_MQ, C_MK, C_MV = 3336, 3848, 4360
C_DQ, C_DK, C_DV = 4872, 5384, 5896
C_G = 6408


class Buf:
    __slots__ = ("name", "last_w", "readers", "excl")

    def __init__(self, name, excl=False):
        self.name = name
        self.excl = excl
        self.last_w = None
        self.readers = []


class Op:
    __slots__ = ("eng", "fn", "deps", "dma", "sem", "target", "needs_sig", "sig", "idx")


class Prog:
    ENGS = ("pe", "act", "dve", "pool", "sp")
    NDMA = {"sp": 12, "pool": 12, "act": 6}

    def __init__(self, nc):
        self.nc = nc
        self.ops = {e: [] for e in self.ENGS}
        self.nops = 0
        self.dma_hist = {e: [] for e in self.NDMA}
        self.dma_n = {e: 0 for e in self.NDMA}
        self.dma_cnt = {e: [0] * n for e, n in self.NDMA.items()}
        self.pending_barrier = {e: [] for e in self.ENGS}
        self.all_dma = []

    def op(self, eng, fn, reads=(), writes=(), dma=False):
        o = Op()
        o.eng = eng
        o.fn = fn
        o.dma = dma
        o.needs_sig = False
        o.sig = None
        o.sem = None
        o.target = None
        o.idx = self.nops
        self.nops += 1
        deps = {}
        raw = set()
        for b in reads:
            if b.last_w is not None:
                deps[b.last_w.idx] = b.last_w
                raw.add(b.last_w.idx)
            if b.excl:
                for r in b.readers:
                    deps[r.idx] = r
        for b in writes:
            if b.last_w is not None:
                deps[b.last_w.idx] = b.last_w
            for r in b.readers:
                deps[r.idx] = r
        for d in self.pending_barrier[eng]:
            deps[d.idx] = d
        self.pending_barrier[eng] = []
        if dma:
            k = self.dma_n[eng] % self.NDMA[eng]
            self.dma_n[eng] += 1
            self.dma_cnt[eng][k] += 16
            o.sem = (eng, k)
            o.target = self.dma_cnt[eng][k]
            hist = self.dma_hist[eng]
            if len(hist) >= self.NDMA[eng]:
                prev = hist[-self.NDMA[eng]]
                deps[prev.idx] = prev
            hist.append(o)
            self.all_dma.append(o)
        o.deps = []
        for d in deps.values():
            if d is o:
                continue
            if d.dma or d.eng != eng or (d.idx in raw and eng != "pe"):
                o.deps.append(d)
                d.needs_sig = True
        for b in reads:
            if not dma:
                b.readers = [r for r in b.readers if r.dma or r.eng != eng]
            b.readers.append(o)
        for b in writes:
            b.last_w = o
            b.readers = []
        self.ops[eng].append(o)
        return o

    def barrier(self):
        lasts = []
        for e in self.ENGS:
            for o in reversed(self.ops[e]):
                if not o.dma and o.fn is not None:
                    lasts.append(o)
                    break
        lasts.extend(self.all_dma[-36:])
        for e in self.ENGS:
            self.pending_barrier[e] = list(lasts)

    def emit(self, stack):
        nc = self.nc
        esem = {e: stack.enter_context(nc.semaphore("s_" + e)) for e in self.ENGS}
        dsem = {}
        for e, n in self.NDMA.items():
            for k in range(n):
                dsem[(e, k)] = stack.enter_context(nc.semaphore("d_%s%d" % (e, k)))
        for e in self.ENGS:
            c = 0
            for o in self.ops[e]:
                if o.needs_sig and not o.dma:
                    c += 1
                    o.sig = c
        block = stack.enter_context(nc.Block())

        def run(name):
            def f(eng):
                seen = {}
                for o in self.ops[name]:
                    for d in o.deps:
                        if d.dma:
                            key, val, sem = d.sem, d.target, dsem[d.sem]
                        else:
                            key, val, sem = d.eng, d.sig, esem[d.eng]
                        if seen.get(key, 0) >= val:
                            continue
                        seen[key] = val
                        eng.wait_ge(sem, val)
                    if o.fn is None:
                        continue
                    ins = o.fn(eng)
                    if o.dma:
                        ins.then_inc(dsem[o.sem], 16)
                    elif o.needs_sig:
                        ins.then_inc(esem[name], 1)
            return f

        block.tensor(run("pe"))
        block.scalar(run("act"))
        block.vector(run("dve"))
        block.gpsimd(run("pool"))
        block.sync(run("sp"))


class _Rec:
    def __getattr__(self, name):
        def mk(*a, **k):
            return lambda e: getattr(e, name)(*a, **k)
        return mk


R = _Rec()


class Arena:
    def __init__(self, tensor, nfloats):
        self.t = tensor
        self.n = nfloats
        self.base = 0
        self.off = 0

    def alloc(self, shape, dt=F32):
        n = int(np.prod(shape))
        nf = n if dt in (F32, U32) else (n + 1) // 2
        nf = (nf + 7) // 8 * 8
        assert self.off + nf <= self.n, ("SBUF arena overflow", self.off, nf, self.n)
        ap = self.t[:, self.off:self.off + nf]
        self.off += nf
        if dt != F32:
            ap = ap.bitcast(dt)
        ap = ap[:, 0:n]
        if len(shape) == 2:
            ap = ap.rearrange("p (a b) -> p a b", a=shape[0])
        elif len(shape) == 3:
            ap = ap.rearrange("p (a b c) -> p a b c", a=shape[0], b=shape[1])
        return ap

    def mark(self):
        return self.off

    def reset(self, m):
        self.off = m


def _consts():
    c = {}
    c["ident"] = np.eye(128, dtype=np.float32)
    half = 32
    inv_freq = (10000.0 ** (-np.arange(half, dtype=np.float32) / half)).astype(np.float32)
    ang = np.arange(S, dtype=np.float32)[:, None] * inv_freq[None, :]
    cos = np.cos(ang).astype(np.float32).T
    sin = np.sin(ang).astype(np.float32).T
    cos64 = np.concatenate([cos, cos], 0)
    sin64 = np.concatenate([-sin, sin], 0)
    c["ropec"] = np.ascontiguousarray(np.concatenate([cos64, cos64], 0))
    c["ropes"] = np.ascontiguousarray(np.concatenate([sin64, sin64], 0))
    s = np.arange(128)[:, None]
    l = np.arange(512)[None, :]
    c["trim"] = np.stack([(128 * j + s <= l) for j in range(4)], 1).astype(np.float32)
    lg = np.log1p(-np.exp2(-5.0 - np.arange(8, dtype=np.float64)))
    c["rett"] = np.stack([np.exp((l - s) * lg[h]) for h in range(8)], 1).astype(np.float32)
    dm = []
    for dlt in list(range(-3, 5)) + [5]:
        dist = 128 * dlt + l - s
        m = ((dist >= 0) & (dist <= 128)).astype(np.float32)
        m += ((dist >= 0) & (dist <= 512) & (dist % 4 == 0)).astype(np.float32)
        m += ((dist >= 0) & (dist <= 2048) & (dist % 16 == 0)).astype(np.float32)
        dm.append(m)
    c["dilm"] = np.stack(dm, 1).astype(np.float32)
    x = np.arange(3968)[None, :]
    c["cumm"] = (x >= s + 1920).astype(np.float32)
    sel = np.zeros((8, 8, 128), np.float32)
    for h in range(8):
        sel[h, h, :] = 1.0
    c["hsel"] = sel.transpose(1, 0, 2).copy()
    misc = np.zeros((128, 256), np.float32)
    misc[:, 0:128] = np.arange(128)[None, :]
    misc[:, 128:143] = 16.0 * np.arange(1, 16)[None, :]
    misc[:, 144:160] = np.arange(16)[None, :]
    misc[:, 160] = EPS
    misc[:, 161] = 1.0
    c["misc"] = misc
    return c


def _swap_halves(w):
    w = w.reshape(w.shape[0], 8, 2, 32)
    return np.ascontiguousarray(w[:, :, ::-1, :].reshape(w.shape[0], 512))


def build(dbg=None):
    nc = bass.Bass("TRN2", target_bir_lowering=False)

    def din(name, shape, dt=F32):
        return nc.dram_tensor(name, list(shape), dt, kind="ExternalInput").ap()

    x_d = din("x", [S, D])
    mem_d = din("mem", [256, D])
    w_in_d = din("w_in", [DEPTH, D, IN_DIM])
    w_sw_d = din("w_sw", [DEPTH, D, 6 * 512])
    convw_d = din("ssd_conv_w", [DEPTH, 4, 768])
    convb_d = din("ssd_conv_b", [DEPTH, 768])
    dtb_d = din("ssd_dt_bias", [DEPTH, 8])
    alog_d = din("ssd_a_log", [DEPTH, 8])
    sd_d = din("ssd_d", [DEPTH, 8])
    snorm_d = din("ssd_norm", [DEPTH, 512])
    rnorm_d = din("ret_norm", [DEPTH, 512])
    wbr_d = din("w_branch", [DEPTH, 4, 512, D])
    wout_d = din("w_out", [DEPTH, D, D])
    mixn_d = din("mix_norm", [DEPTH, D])
    xn_d = din("x_norm", [DEPTH, D])
    wxq_d = din("w_xq", [DEPTH, D, D])
    wxkv_d = din("w_xkv", [DEPTH, D, 2 * D])
    wxo_d = din("w_xo", [DEPTH, D, D])
    fn_d = din("ffn_norm", [DEPTH, D])
    wpq_d = din("w_pq", [DEPTH, D, 2048])
    skT_d = din("skT", [DEPTH, 16, 128, 128])
    uT_d = din("peer_uT", [DEPTH, D, 16384])
    pv_d = din("peer_v", [DEPTH, 16384, D])
    fnorm_d = din("final_norm", [1, D])
    cd = {k: din("c_" + k, v.shape) for k, v in _consts().items()}

    out_d = nc.dram_tensor("out", [S, D], F32, kind="ExternalOutput").ap()
    okind = "ExternalOutput" if dbg else "Internal"
    hbuf = nc.dram_tensor("hbuf", [S, D], F32, kind=okind).ap()
    ymix = nc.dram_tensor("ymix", [4, S, 512], F32, kind=okind).ap()
    wd = nc.dram_tensor("wd", [128, 128, S], BF16, kind="Internal").ap()
    dbg_d = nc.dram_tensor("dbgbuf", [128, 4096], F32, kind=okind).ap()

    st = ExitStack()
    arena_t = st.enter_context(nc.sbuf_tensor("arena", [128, 200 * 256], F32))
    banks = [st.enter_context(nc.psum_tensor("bank%d" % i, [128, 512], F32)) for i in range(8)]
    A = Arena(arena_t, 200 * 256)
    p = Prog(nc)
    PB = [Buf("ps%d" % i, excl=True) for i in range(8)]

    def bank_bf(i):
        return banks[i][:, :].bitcast(BF16)

    hnT = A.alloc((8, S), BF16)
    B_hnT = Buf("hnT")
    ident_f = A.alloc((128,))
    ident_b = A.alloc((128,), BF16)
    misc = A.alloc((256,))
    trim = A.alloc((4, 512), BF16)
    B_const = Buf("const")
    iota_row = misc[:, 0:128]
    thr15 = misc[:, 128:143]
    iota16 = misc[:, 144:160]
    eps_t = misc[:, 160:161]
    ones_t = misc[:, 161:162]
    p.op("sp", R.dma_start(out=ident_f, in_=cd["ident"]), writes=[B_const], dma=True)
    p.op("pool", R.dma_start(out=ident_b, in_=cd["ident"]), writes=[B_const], dma=True)
    p.op("sp", R.dma_start(out=misc, in_=cd["misc"]), writes=[B_const], dma=True)
    p.op("pool", R.dma_start(out=trim, in_=cd["trim"]), writes=[B_const], dma=True)
    PERSIST = A.mark()

    B_h = Buf("hbuf")
    B_ymix = [Buf("ymix%d" % i) for i in range(4)]

    def chain(eng, fns, reads, scratch):
        for f in fns:
            p.op(eng, f, reads=list(reads) + list(scratch), writes=list(scratch))

    B_dbg = Buf("dbg")

    def tap(ap, bufs, col0):
        if not dbg:
            return
        n = ap.shape[1]
        pp = ap.shape[0]
        p.op("sp", R.dma_start(out=dbg_d[0:pp, col0:col0 + n], in_=ap), reads=bufs, writes=[B_dbg], dma=True)

    def wload(dst, src2d, eng="pool"):
        return R.dma_start(out=dst, in_=src2d.rearrange("(kc p) n -> p kc n", p=128))

    def bcast_row(dst, row_ap, bufs, eng="sp"):
        p.op(eng, R.dma_start(out=dst, in_=row_ap.partition_broadcast(128)), writes=bufs, dma=True)

    def norm_phase(gain_row, src=None, final_out=None):
        p.barrier()
        A.reset(PERSIST)
        gain_bc = A.alloc((D,))
        Bg = Buf("gain")
        bcast_row(gain_bc, gain_row, [Bg])
        hts = [A.alloc((D,)) for _ in range(2)]
        Bht = [Buf("ht%d" % i) for i in range(2)]
        junk = A.alloc((D,), BF16)
        hnb = [A.alloc((D,), BF16) for _ in range(2)]
        Bhnb = [Buf("hnb%d" % i) for i in range(2)]
        outs = [A.alloc((D,)) for _ in range(2)]
        Bouts = [Buf("no%d" % i) for i in range(2)]
        ss = A.alloc((NT,))
        rs = A.alloc((NT,))
        Bss = Buf("ss")
        srcd = hbuf if src is None else src
        for j in range(NT):
            ht, bh = hts[j % 2], Bht[j % 2]
            rows = slice(j * 128, (j + 1) * 128)
            p.op("sp", R.dma_start(out=ht, in_=srcd[rows, :]),
                 reads=[B_h], writes=[bh], dma=True)
            if src is not None:
                p.op("sp", R.dma_start(out=hbuf[rows, :], in_=ht),
                     reads=[bh], writes=[B_h], dma=True)
            p.op("act", R.activation(out=junk, in_=ht, func=AF.Square, accum_out=ss[:, j:j + 1]),
                 reads=[bh], writes=[Bss])
            p.op("act", R.activation(out=rs[:, j:j + 1], in_=ss[:, j:j + 1], func=AF.Sqrt,
                                                      bias=eps_t, scale=1.0 / D),
                 reads=[Bss, B_const], writes=[Bss])
            p.op("dve", R.reciprocal(out=rs[:, j:j + 1], in_=rs[:, j:j + 1]), reads=[Bss], writes=[Bss])
            if final_out is not None:
                o, bo = outs[j % 2], Bouts[j % 2]
                p.op("dve", R.scalar_tensor_tensor(
                    out=o, in0=ht, scalar=rs[:, j:j + 1], in1=gain_bc, op0=ALU.mult, op1=ALU.mult),
                    reads=[bh, Bss, Bg], writes=[bo])
                p.op("sp", R.dma_start(out=final_out[rows, :], in_=o),
                     reads=[bo], writes=[B_out], dma=True)
                continue
            hb, bhb = hnb[j % 2], Bhnb[j % 2]
            p.op("dve", R.scalar_tensor_tensor(
                out=hb, in0=ht, scalar=rs[:, j:j + 1], in1=gain_bc, op0=ALU.mult, op1=ALU.mult),
                reads=[bh, Bss, Bg], writes=[bhb])
            pb = 6 + (j % 2)
            pst = bank_bf(pb)
            for c in range(8):
                p.op("pe", R.transpose(
                    out=pst[:, c * 128:(c + 1) * 128], in_=hb[:, c * 128:(c + 1) * 128], identity=ident_b),
                    reads=[bhb, B_const], writes=[PB[pb]])
            p.op("act", R.activation(
                out=hnT[:, :, j * 128:(j + 1) * 128], in_=pst.rearrange("p (c t) -> p c t", c=8), func=AF.Copy),
                reads=[PB[pb]], writes=[B_hnT])

    B_out = Buf("out")

    def proj_fm(w_ap, tg, pb, wbuf, n=512, t0=None):
        t0 = tg * 512 if t0 is None else t0
        for kc in range(8):
            p.op("pe", R.matmul(banks[pb][:w_ap.shape[2], 0:n], lhsT=w_ap[:, kc, :],
                                                  rhs=hnT[:, kc, t0:t0 + n], start=(kc == 0), stop=(kc == 7)),
                 reads=[wbuf, B_hnT], writes=[PB[pb]])

    def proj_tm(w_ap, j, pb, wbuf, n=512):
        for kc in range(8):
            p.op("pe", R.matmul(banks[pb][:, 0:n], lhsT=hnT[:, kc, j * 128:(j + 1) * 128],
                                                  rhs=w_ap[:, kc, 0:n], start=(kc == 0), stop=(kc == 7)),
                 reads=[wbuf, B_hnT], writes=[PB[pb]])

    def attn_core(QT, KT, Vt, nv, bufs_in, make_E, finish, khead=lambda h: h, moba=None):
        Es = [A.alloc((512,), BF16) for _ in range(3)]
        BE = [Buf("E%d" % i) for i in range(3)]
        cnt = 0
        ocnt = 0
        for qg in range(4):
            for h in range(8):
                kh = khead(h)
                qch, qp = h // 2, (h % 2) * 64
                kch, kp = kh // 2, (kh % 2) * 64
                ob = 2 + (ocnt % 2)
                ocnt += 1
                O = banks[ob][:, 0:4 * 65].rearrange("p (q v) -> p q v", q=4)
                started = [False] * 4
                nkt = 4 * qg + 4
                last_kt = [4 * qg + qt for qt in range(4)]
                for kt in range(nkt):
                    sb = cnt % 2
                    E, bE = Es[cnt % 3], BE[cnt % 3]
                    cnt += 1
                    p.op("pe", R.matmul(
                        banks[sb][:, 0:512], lhsT=KT[kp:kp + 64, kch, kt * 128:(kt + 1) * 128],
                        rhs=QT[qp:qp + 64, qch, qg * 512:(qg + 1) * 512], start=True, stop=True),
                        reads=bufs_in, writes=[PB[sb]])
                    make_E(h, qg, kt, banks[sb][:, 0:512], PB[sb], E, bE)
                    for qt in range(4):
                        if kt > last_kt[qt]:
                            continue
                        p.op("pe", R.matmul(
                            O[:, qt, 0:nv], lhsT=E[:, qt * 128:(qt + 1) * 128], rhs=Vt[:, kt, h, 0:nv],
                            start=(not any(started)), stop=(kt == last_kt[qt]), skip_group_check=True),
                            reads=[bE] + bufs_in, writes=[PB[ob]])
                        started[qt] = True
                finish(qg, h, O, PB[ob])

    def proj_rope(dst, bdst, w, wsw, bw, ropec, ropes, Brope, tmp1, tmp2, Btmp):
        for c in range(4):
            for tg in range(4):
                proj_fm(w[:, :, c * 128:(c + 1) * 128], tg, 4, bw)
                proj_fm(wsw[:, :, c * 128:(c + 1) * 128], tg, 5, bw)
                sl = slice(tg * 512, (tg + 1) * 512)
                p.op("dve", R.tensor_tensor(out=tmp1, in0=banks[4][:, 0:512], in1=ropec[:, sl], op=ALU.mult),
                     reads=[PB[4], Brope], writes=[Btmp])
                p.op("dve", R.tensor_tensor(out=tmp2, in0=banks[5][:, 0:512], in1=ropes[:, sl], op=ALU.mult),
                     reads=[PB[5], Brope], writes=[Btmp])
                p.op("dve", R.tensor_tensor(out=dst[:, c, sl], in0=tmp1, in1=tmp2, op=ALU.add),
                     reads=[Btmp], writes=[bdst])

    def proj_v(Vt, bV, w, bw, nones):
        if nones:
            p.op("dve", R.memset(Vt[:, :, :, 64:65], 1.0), writes=[bV])
        for j in range(NT):
            pb = 4 + (j % 2)
            proj_tm(w, j, pb, bw)
            p.op("act", R.activation(
                out=Vt[:, j, :, 0:64], in_=banks[pb][:, 0:512].rearrange("p (h v) -> p h v", h=8), func=AF.Copy),
                reads=[PB[pb]], writes=[bV])

    def store_y(i, qg, ytile, by):
        p.op("sp", R.dma_start(
            out=ymix[i, qg * 512:(qg + 1) * 512, :].rearrange("(q p) c -> p q c", p=128), in_=ytile),
            reads=[by], writes=[B_ymix[i]], dma=True)

    def softmax_mixer(L, kind):
        p.barrier()
        A.reset(PERSIST)
        cq, ck, cv = (C_MQ, C_MK, C_MV) if kind == 2 else (C_DQ, C_DK, C_DV)
        swq = (2 if kind == 2 else 4) * 512
        Bw = Buf("w")
        wq = A.alloc((8, 512), BF16); wqs = A.alloc((8, 512), BF16)
        wk = A.alloc((8, 512), BF16); wks = A.alloc((8, 512), BF16)
        wv = A.alloc((8, 512), BF16)
        p.op("pool", wload(wq, w_in_d[L, :, cq:cq + 512]), writes=[Bw], dma=True)
        p.op("pool", wload(wqs, w_sw_d[L, :, swq:swq + 512]), writes=[Bw], dma=True)
        p.op("pool", wload(wk, w_in_d[L, :, ck:ck + 512]), writes=[Bw], dma=True)
        p.op("pool", wload(wks, w_sw_d[L, :, swq + 512:swq + 1024]), writes=[Bw], dma=True)
        p.op("pool", wload(wv, w_in_d[L, :, cv:cv + 512]), writes=[Bw], dma=True)
        ropec = A.alloc((S,)); ropes = A.alloc((S,))
        Brope = Buf("rope")
        p.op("sp", R.dma_start(out=ropec, in_=cd["ropec"]), writes=[Brope], dma=True)
        p.op("sp", R.dma_start(out=ropes, in_=cd["ropes"]), writes=[Brope], dma=True)
        QT = A.alloc((4, S), BF16); KT = A.alloc((4, S), BF16)
        BQ, BK, BV = Buf("QT"), Buf("KT"), Buf("Vt")
        Vt = A.alloc((NT, 8, 65), BF16)
        tmp1 = A.alloc((512,)); tmp2 = A.alloc((512,))
        Btmp = Buf("tmp")
        proj_rope(QT, BQ, wq, wqs, Bw, ropec, ropes, Brope, tmp1, tmp2, Btmp)
        proj_rope(KT, BK, wk, wks, Bw, ropec, ropes, Brope, tmp1, tmp2, Btmp)
        proj_v(Vt, BV, wv, Bw, True)
        ytiles = [A.alloc((4, 512)) for _ in range(2)]
        Byt = [Buf("yt%d" % i) for i in range(2)]
        rc = A.alloc((4,))
        Brc = Buf("rc")
        if kind == 3:
            dilm = A.alloc((9, 512), BF16)
            Bdm = Buf("dilm")
            p.op("pool", R.dma_start(out=dilm, in_=cd["dilm"]), writes=[Bdm], dma=True)

            def make_E(h, qg, kt, psS, bS, E, bE):
                p.op("act", R.activation(out=E, in_=psS, func=AF.Exp, scale=0.125), reads=[bS], writes=[bE])
                dlt = 4 * qg - kt
                mi = 8 if dlt >= 5 else dlt + 3
                p.op("dve", R.tensor_tensor(out=E, in0=E, in1=dilm[:, mi, :], op=ALU.mult),
                     reads=[bE, Bdm], writes=[bE])

            def finish(qg, h, O, bO):
                yt, by = ytiles[qg % 2], Byt[qg % 2]
                p.op("dve", R.reciprocal(out=rc, in_=O[:, :, 64]), reads=[bO], writes=[Brc])
                p.op("dve", R.tensor_tensor(
                    out=yt[:, :, h * 64:(h + 1) * 64], in0=O[:, :, 0:64],
                    in1=rc.unsqueeze(2).to_broadcast([128, 4, 64]), op=ALU.mult),
                    reads=[bO, Brc], writes=[by])
                if h == 7:
                    store_y(kind, qg, yt, by)

            attn_core(QT, KT, Vt, 65, [BQ, BK, BV], make_E, finish)
            return

        if dbg and dbg.get('cut') == 5:
            return
        ksum = A.alloc((4, 8), BF16)
        ksum_f = A.alloc((4, 8))
        Bks = Buf("ksum")
        for c in range(4):
            p.op("dve", R.tensor_reduce(
                out=ksum_f[:, c, :], in_=KT[:, c, :].rearrange("p (b t) -> p b t", b=8), axis=AX.X, op=ALU.add),
                reads=[BK], writes=[Bks])
        p.op("dve", R.tensor_copy(out=ksum, in_=ksum_f), reads=[Bks], writes=[Bks])
        selm = A.alloc((NT, 8, 8))
        Bsel = Buf("selm")
        g = A.alloc((8, 8)); g2 = A.alloc((8, 8)); eq = A.alloc((8, 8)); mx = A.alloc((8,))
        Bg_ = Buf("gate")
        for j in range(8, NT):
            nb = j // 2
            for h in range(8):
                gb = 6 + (h % 2)
                p.op("pe", R.matmul(
                    banks[gb][:, (h // 2) * 8:(h // 2) * 8 + 8], lhsT=QT[(h % 2) * 64:(h % 2) * 64 + 64, h // 2, j * 128:(j + 1) * 128],
                    rhs=ksum[(h % 2) * 64:(h % 2) * 64 + 64, h // 2, :], start=True, stop=True),
                    reads=[BQ, Bks], writes=[PB[gb]])
            g4 = g.rearrange("p (a two) b -> p a two b", two=2)
            for par in range(2):
                p.op("dve", R.tensor_copy(out=g4[:, :, par, 0:nb],
                                          in_=banks[6 + par][:, 0:32].rearrange("p (a b) -> p a b", a=4)[:, :, 0:nb]),
                     reads=[PB[6 + par], Bg_], writes=[Bg_])
            gv, g2v, eqv = g[:, :, 0:nb], g2[:, :, 0:nb], eq[:, :, 0:nb]
            mxb = mx.unsqueeze(2).to_broadcast([128, 8, nb])
            fns = []
            cur = gv
            for r in range(2):
                fns.append(R.tensor_reduce(out=mx, in_=cur, axis=AX.X, op=ALU.max))
                fns.append(R.tensor_tensor(out=eqv, in0=cur, in1=mxb, op=ALU.is_ge))
                fns.append(R.scalar_tensor_tensor(out=g2v, in0=eqv, scalar=NEG, in1=cur, op0=ALU.mult, op1=ALU.add))
                cur = g2v
            fns.append(R.tensor_reduce(out=mx, in_=cur, axis=AX.X, op=ALU.max))
            fns.append(R.tensor_tensor(out=selm[:, j, :, 0:nb], in0=gv, in1=mxb, op=ALU.is_ge))
            chain("dve", fns, [], [Bsel, Bg_])

        if dbg and dbg.get('cut') == 4:
            return
        acc = A.alloc((4, 65))
        Bacc = Buf("acc")
        Es = [A.alloc((512,), BF16) for _ in range(3)]
        BE = [Buf("E%d" % i) for i in range(3)]
        cnt = 0
        ocnt = 0
        for qg in range(4):
            yt, by = ytiles[qg % 2], Byt[qg % 2]
            for h in range(8):
                hp, hc = (h % 2) * 64, h // 2
                p.op("dve", R.memset(acc, 0.0), writes=[Bacc])
                for kb in range(2 * qg + 2):
                    ob = 2 + (ocnt % 2)
                    ocnt += 1
                    O = banks[ob][:, 0:4 * 65].rearrange("p (q v) -> p q v", q=4)
                    started = [False] * 4
                    use = []
                    for qt in range(4):
                        bq = (4 * qg + qt) // 2
                        if kb <= bq:
                            use.append(qt)
                    for kk in range(2):
                        kt = 2 * kb + kk
                        sb = cnt % 2
                        E, bE = Es[cnt % 3], BE[cnt % 3]
                        cnt += 1
                        p.op("pe", R.matmul(
                            banks[sb][:, 0:512], lhsT=KT[hp:hp + 64, hc, kt * 128:(kt + 1) * 128],
                            rhs=QT[hp:hp + 64, hc, qg * 512:(qg + 1) * 512], start=True, stop=True),
                            reads=[BQ, BK], writes=[PB[sb]])
                        p.op("act", R.activation(out=E, in_=banks[sb][:, 0:512], func=AF.Exp, scale=0.125),
                             reads=[PB[sb]], writes=[bE])
                        dj = kt - 4 * qg
                        if dj >= 0:
                            p.op("dve", R.tensor_tensor(out=E, in0=E, in1=trim[:, dj, :], op=ALU.mult),
                                 reads=[bE, B_const], writes=[bE])
                        for qt in use:
                            if kt > 4 * qg + qt:
                                continue
                            lastk = min(2 * kb + 1, 4 * qg + qt)
                            p.op("pe", R.matmul(
                                O[:, qt, :], lhsT=E[:, qt * 128:(qt + 1) * 128], rhs=Vt[:, kt, h, :],
                                start=(not any(started)), stop=(kt == lastk), skip_group_check=True),
                                reads=[bE, BV], writes=[PB[ob]])
                            started[qt] = True
                    for qt in use:
                        jt = 4 * qg + qt
                        bq = jt // 2
                        if kb == bq or bq <= 3:
                            p.op("dve", R.tensor_tensor(out=acc[:, qt, :], in0=O[:, qt, :], in1=acc[:, qt, :], op=ALU.add),
                                 reads=[PB[ob], Bacc], writes=[Bacc])
                        else:
                            p.op("dve", R.scalar_tensor_tensor(
                                out=acc[:, qt, :], in0=O[:, qt, :], scalar=selm[:, jt, h, kb:kb + 1], in1=acc[:, qt, :],
                                op0=ALU.mult, op1=ALU.add),
                                reads=[PB[ob], Bacc, Bsel], writes=[Bacc])
                p.op("dve", R.reciprocal(out=rc, in_=acc[:, :, 64]), reads=[Bacc], writes=[Brc])
                p.op("dve", R.tensor_tensor(
                    out=yt[:, :, h * 64:(h + 1) * 64], in0=acc[:, :, 0:64],
                    in1=rc.unsqueeze(2).to_broadcast([128, 4, 64]), op=ALU.mult),
                    reads=[Bacc, Brc], writes=[by])
            store_y(2, qg, yt, by)

    def ret_mixer(L):
        p.barrier()
        A.reset(PERSIST)
        Bw = Buf("w")
        wq = A.alloc((8, 512), BF16); wqs = A.alloc((8, 512), BF16)
        wk = A.alloc((8, 512), BF16); wks = A.alloc((8, 512), BF16)
        wv = A.alloc((8, 512), BF16); wg = A.alloc((8, 512), BF16)
        p.op("pool", wload(wq, w_in_d[L, :, C_RQ:C_RQ + 512]), writes=[Bw], dma=True)
        p.op("pool", wload(wqs, w_sw_d[L, :, 0:512]), writes=[Bw], dma=True)
        p.op("pool", wload(wk, w_in_d[L, :, C_RK:C_RK + 512]), writes=[Bw], dma=True)
        p.op("pool", wload(wks, w_sw_d[L, :, 512:1024]), writes=[Bw], dma=True)
        p.op("pool", wload(wv, w_in_d[L, :, C_RV:C_RV + 512]), writes=[Bw], dma=True)
        p.op("pool", wload(wg, w_in_d[L, :, C_RG:C_RG + 512]), writes=[Bw], dma=True)
        ropec = A.alloc((S,)); ropes = A.alloc((S,))
        Brope = Buf("rope")
        p.op("sp", R.dma_start(out=ropec, in_=cd["ropec"]), writes=[Brope], dma=True)
        p.op("sp", R.dma_start(out=ropes, in_=cd["ropes"]), writes=[Brope], dma=True)
        rett = A.alloc((8, 512))
        Brt = Buf("rett")
        p.op("sp", R.dma_start(out=rett, in_=cd["rett"]), writes=[Brt], dma=True)
        gain_bc = A.alloc((512,))
        Bgn = Buf("gain")
        bcast_row(gain_bc, rnorm_d[L:L + 1, :], [Bgn])
        QT = A.alloc((4, S), BF16); KT = A.alloc((4, S), BF16)
        BQ, BK, BV = Buf("QT"), Buf("KT"), Buf("Vt")
        Vt = A.alloc((NT, 8, 65), BF16)
        tmp1 = A.alloc((512,)); tmp2 = A.alloc((512,))
        Btmp = Buf("tmp")
        proj_rope(QT, BQ, wq, wqs, Bw, ropec, ropes, Brope, tmp1, tmp2, Btmp)
        proj_rope(KT, BK, wk, wks, Bw, ropec, ropes, Brope, tmp1, tmp2, Btmp)
        proj_v(Vt, BV, wv, Bw, False)
        ytiles = [A.alloc((4, 512)) for _ in range(2)]
        Byt = [Buf("yt%d" % i) for i in range(2)]
        lg = np.log1p(-np.exp2(-5.0 - np.arange(8, dtype=np.float64)))

        def make_E(h, qg, kt, psS, bS, E, bE):
            dlt = 4 * qg - kt
            cst = float(np.exp(128.0 * dlt * lg[h]) * 0.125)
            p.op("dve", R.scalar_tensor_tensor(out=E, in0=psS, scalar=cst, in1=rett[:, h, :],
                                                         op0=ALU.mult, op1=ALU.mult),
                 reads=[bS, Brt], writes=[bE])
            if dlt <= 0:
                p.op("dve", R.tensor_tensor(out=E, in0=E, in1=trim[:, -dlt, :], op=ALU.mult),
                     reads=[bE, B_const], writes=[bE])

        mean = A.alloc((8,)); var = A.alloc((8,)); cen = A.alloc((512,)); sq = A.alloc((512,)); sg = A.alloc((512,))
        Bpp = Buf("pp")
        Bsg2 = Buf("sg2")

        def finish(qg, h, O, bO):
            yt, by = ytiles[qg % 2], Byt[qg % 2]
            p.op("act", R.activation(out=yt[:, :, h * 64:(h + 1) * 64], in_=O[:, :, 0:64], func=AF.Copy),
                 reads=[bO], writes=[by])
            if h != 7:
                return
            for qt in range(4):
                j = 4 * qg + qt
                y = yt[:, qt, :]
                y3 = y.rearrange("p (h v) -> p h v", h=8)
                proj_tm(wg, j, 4 + (qt % 2), Bw)
                gp = banks[4 + (qt % 2)][:, 0:512]

                c3 = cen.rearrange("p (h v) -> p h v", h=8)
                chain("dve", [
                    R.tensor_reduce(out=mean, in_=y3, axis=AX.X, op=ALU.add),
                    R.tensor_scalar(out=mean, in0=mean, scalar1=1.0 / 64, scalar2=None, op0=ALU.mult),
                    R.tensor_tensor(out=c3, in0=y3, in1=mean.unsqueeze(2).to_broadcast([128, 8, 64]), op=ALU.subtract),
                    R.tensor_tensor(out=sq, in0=cen, in1=cen, op=ALU.mult),
                    R.tensor_reduce(out=var, in_=sq.rearrange("p (h v) -> p h v", h=8), axis=AX.X, op=ALU.add),
                ], [by], [Bpp])
                p.op("act", R.activation(out=var, in_=var, func=AF.Sqrt, bias=eps_t, scale=1.0 / 64),
                     reads=[Bpp, B_const], writes=[Bpp])
                p.op("act", R.activation(out=sg, in_=gp, func=AF.Silu),
                     reads=[PB[4 + (qt % 2)]], writes=[Bsg2])
                chain("dve", [
                    R.reciprocal(out=var, in_=var),
                    R.tensor_tensor(out=c3, in0=c3, in1=var.unsqueeze(2).to_broadcast([128, 8, 64]), op=ALU.mult),
                    R.tensor_tensor(out=cen, in0=cen, in1=gain_bc, op=ALU.mult),
                    R.tensor_tensor(out=y, in0=cen, in1=sg, op=ALU.mult),
                ], [Bgn, Bsg2], [Bpp, by])
            store_y(1, qg, yt, by)

        attn_core(QT, KT, Vt, 64, [BQ, BK, BV], make_E, finish)

    def ssd_mixer(L):
        p.barrier()
        A.reset(PERSIST)
        Bw = Buf("w")
        wz = A.alloc((8, 512), BF16)
        wx = A.alloc((8, 768), BF16)
        wdt = A.alloc((8, 8), BF16)
        p.op("pool", wload(wz, w_in_d[L, :, C_Z:C_Z + 512]), writes=[Bw], dma=True)
        p.op("pool", wload(wx, w_in_d[L, :, C_XBC:C_XBC + 768]), writes=[Bw], dma=True)
        p.op("pool", wload(wdt, w_in_d[L, :, C_DT:C_DT + 8]), writes=[Bw], dma=True)
        cw = A.alloc((6, 4)); cb = A.alloc((6,))
        Bcw = Buf("cw")
        for k in range(4):
            p.op("sp", R.dma_start(out=cw[:, :, k], in_=convw_d[L, k].rearrange("(c p) -> p c", p=128),
                                                  allow_slow_non_contiguous=True), writes=[Bcw], dma=True)
        p.op("sp", R.dma_start(out=cb, in_=convb_d[L].rearrange("(c p) -> p c", p=128),
                                         allow_slow_non_contiguous=True), writes=[Bcw], dma=True)
        small = A.alloc((4, 8))
        Bsm = Buf("small")
        bcast_row(small[:, 0, :], dtb_d[L:L + 1, :], [Bsm])
        bcast_row(small[:, 1, :], alog_d[L:L + 1, :], [Bsm])
        bcast_row(small[:, 2, :], sd_d[L:L + 1, :], [Bsm])
        p.op("act", R.activation(out=small[:, 1, :], in_=small[:, 1, :], func=AF.Exp), reads=[Bsm], writes=[Bsm])
        gain_bc = A.alloc((512,))
        Bgn = Buf("gain")
        bcast_row(gain_bc, snorm_d[L:L + 1, :], [Bgn])
        cumm = A.alloc((3968,))
        hsel = A.alloc((8, 128))
        Bcm = Buf("cumm")
        p.op("sp", R.dma_start(out=cumm, in_=cd["cumm"]), writes=[Bcm], dma=True)
        p.op("sp", R.dma_start(out=hsel[0:8], in_=cd["hsel"]), writes=[Bcm], dma=True)

        if dbg and dbg.get('cut') == 1:
            return
        stage = A.alloc((4 + S,))
        cacc = A.alloc((S,))
        Bst, Bca = Buf("stage"), Buf("cacc")
        xs_f = A.alloc((NT, 512))
        Vt = A.alloc((NT, 8, 65), BF16)
        BT = A.alloc((S,), BF16); CT = A.alloc((S,), BF16)
        BV, BBC, Bxf = Buf("Vt"), Buf("BC"), Buf("xsf")
        p.op("dve", R.memset(stage[:, 0:4], 0.0), writes=[Bst])
        for c in range(6):
            for tg in range(4):
                pb = 4 + (tg % 2)
                proj_fm(wx[:, :, c * 128:(c + 1) * 128], tg, pb, Bw)
                p.op("act", R.activation(out=stage[:, 4 + tg * 512:4 + (tg + 1) * 512],
                                                                 in_=banks[pb][:, 0:512], func=AF.Copy),
                     reads=[PB[pb]], writes=[Bst])

            fns = [R.tensor_scalar(out=cacc, in0=stage[:, 1:1 + S], scalar1=cw[:, c, 0:1], scalar2=cb[:, c:c + 1],
                                                  op0=ALU.mult, op1=ALU.add)]
            for k in range(1, 4):
                fns.append(R.scalar_tensor_tensor(out=cacc, in0=stage[:, 1 + k:1 + k + S], scalar=cw[:, c, k:k + 1],
                                                                      in1=cacc, op0=ALU.mult, op1=ALU.add))
            chain("dve", fns, [Bst, Bcw], [Bca])
            if c < 4:
                p.op("act", R.activation(out=cacc, in_=cacc, func=AF.Silu), reads=[Bca], writes=[Bca])
                for j in range(NT):
                    pb = 6 + (j % 2)
                    p.op("pe", R.transpose(out=banks[pb][:, 0:128], in_=cacc[:, j * 128:(j + 1) * 128],
                                                                 identity=ident_f),
                         reads=[Bca, B_const], writes=[PB[pb]])
                    p.op("act", R.activation(
                        out=Vt[:, j, 2 * c:2 * c + 2, 0:64], in_=banks[pb][:, 0:128].rearrange("p (h v) -> p h v", h=2), func=AF.Copy),
                        reads=[PB[pb]], writes=[BV])
                    p.op("dve", R.tensor_copy(out=xs_f[:, j, c * 128:(c + 1) * 128], in_=banks[pb][:, 0:128]),
                         reads=[PB[pb]], writes=[Bxf])
            else:
                dst = BT if c == 4 else CT
                p.op("act", R.activation(out=dst, in_=cacc, func=AF.Silu), reads=[Bca], writes=[BBC])

        if dbg and dbg.get('cut') == 2:
            return
        dt_tm = A.alloc((NT, 8)); la = A.alloc((NT, 8)); negA = A.alloc((NT, 8))
        A_fm = A.alloc((S,))
        Bdt, BAf = Buf("dt"), Buf("Afm")
        for j in range(NT):
            for kc in range(8):
                p.op("pe", R.matmul(banks[6][:, j * 8:(j + 1) * 8], lhsT=hnT[:, kc, j * 128:(j + 1) * 128],
                                                        rhs=wdt[:, kc, :], start=(kc == 0), stop=(kc == 7), skip_group_check=True),
                     reads=[Bw, B_hnT], writes=[PB[6]])

        def dtf(e):
            e.tensor_tensor(out=dt_tm, in0=banks[6][:, 0:128].rearrange("p (j h) -> p j h", j=NT),
                            in1=small[:, 0:1, :].to_broadcast([128, NT, 8]), op=ALU.add)
            return e
        p.op("dve", R.tensor_tensor(out=dt_tm, in0=banks[6][:, 0:128].rearrange("p (j h) -> p j h", j=NT),
                                              in1=small[:, 0:1, :].to_broadcast([128, NT, 8]), op=ALU.add),
             reads=[PB[6], Bsm], writes=[Bdt])
        p.op("act", R.activation(out=dt_tm, in_=dt_tm, func=AF.Exp), reads=[Bdt], writes=[Bdt])
        p.op("act", R.activation(out=dt_tm, in_=dt_tm, func=AF.Ln, bias=ones_t, scale=1.0), reads=[Bdt, B_const], writes=[Bdt])
        p.op("dve", R.scalar_tensor_tensor(out=la, in0=dt_tm, scalar=-1.0, in1=small[:, 1:2, :].to_broadcast([128, NT, 8]),
                                                     op0=ALU.mult, op1=ALU.mult),
             reads=[Bdt, Bsm], writes=[Bdt])
        for gq in range(4):
            n_i = 4 * gq + 4
            for i in range(n_i):
                o0 = 1920 - 128 * i + 512 * gq
                p.op("pe", R.matmul(banks[7][0:8, 0:512], lhsT=la[:, i, :], rhs=cumm[:, o0:o0 + 512],
                                                                  start=(i == 0), stop=(i == n_i - 1)),
                     reads=[Bdt, Bcm], writes=[PB[7]])
            p.op("act", R.activation(out=A_fm[0:8, gq * 512:(gq + 1) * 512], in_=banks[7][0:8, 0:512], func=AF.Copy),
                 reads=[PB[7]], writes=[BAf])
        for j in range(NT):
            for i in range(j + 1):
                o0 = 1920 - 128 * i + 128 * j
                p.op("pe", R.matmul(banks[6][:, j * 8:(j + 1) * 8], lhsT=cumm[:, o0:o0 + 128], rhs=la[:, i, :],
                                                             start=(i == 0), stop=(i == j), skip_group_check=True),
                     reads=[Bdt, Bcm], writes=[PB[6]])
        p.op("dve", R.tensor_scalar(out=negA, in0=banks[6][:, 0:128].rearrange("p (j h) -> p j h", j=NT),
                                              scalar1=-1.0, scalar2=None, op0=ALU.mult),
             reads=[PB[6]], writes=[Bdt])

        Btap = A.alloc((512,)); Ctap = A.alloc((512,))
        BBt = Buf("Btap")
        p.op("dve", R.tensor_copy(out=Btap, in_=BT[:, 0:512]), reads=[BBC], writes=[BBt])
        p.op("dve", R.tensor_copy(out=Ctap, in_=CT[:, 0:512]), reads=[BBC], writes=[BBt])
        tap(Btap, [BBt], 3200)
        tap(Ctap, [BBt], 3200 + 256)
        tap(xs_f[:, 0, :], [Bxf], 0)
        tap(dt_tm.rearrange("p j h -> p (j h)"), [Bdt], 512)
        tap(negA.rearrange("p j h -> p (j h)"), [Bdt], 640)
        tap(A_fm[0:8, 0:512], [BAf], 768)
        tap(la.rearrange("p j h -> p (j h)"), [Bdt], 1280)
        if dbg and dbg.get('cut') == 3:
            return
        ytiles = [A.alloc((4, 512)) for _ in range(2)]
        Byt = [Buf("yt%d" % i) for i in range(2)]
        Abc = A.alloc((512,))
        BAbc = Buf("Abc")
        dcl = [A.alloc((512,)) for _ in range(2)]
        Bdcl = [Buf("dcl%d" % i) for i in range(2)]
        ecnt = [0]
        KQ = [None]

        def make_E(h, qg, kt, psS, bS, E, bE):
            if kt == 0:
                p.op("pe", R.matmul(banks[7][:, 0:512], lhsT=hsel[0:8, h, :], rhs=A_fm[0:8, qg * 512:(qg + 1) * 512],
                                              start=True, stop=True), reads=[BAf, Bcm], writes=[PB[7]])
                p.op("act", R.activation(out=Abc, in_=banks[7][:, 0:512], func=AF.Copy), reads=[PB[7]], writes=[BAbc])
            d_, bd = dcl[ecnt[0] % 2], Bdcl[ecnt[0] % 2]
            ecnt[0] += 1
            p.op("dve", R.tensor_scalar(out=d_, in0=Abc, scalar1=negA[:, kt, h:h + 1], scalar2=0.0,
                                                  op0=ALU.add, op1=ALU.min), reads=[BAbc, Bdt], writes=[bd])
            p.op("act", R.activation(out=d_, in_=d_, func=AF.Exp), reads=[bd], writes=[bd])
            p.op("dve", R.scalar_tensor_tensor(out=E, in0=psS, scalar=dt_tm[:, kt, h:h + 1], in1=d_,
                                                         op0=ALU.mult, op1=ALU.mult), reads=[bS, bd, Bdt], writes=[bE])
            dj = kt - 4 * qg
            if dj >= 0:
                p.op("dve", R.tensor_tensor(out=E, in0=E, in1=trim[:, dj, :], op=ALU.mult),
                     reads=[bE, B_const], writes=[bE])
            if h == 0 and qg == 0 and kt == 0:
                tap(d_, [bd], 1536)
                p.op("dve", R.tensor_copy(out=Etap, in_=E), reads=[bE], writes=[BEt])
                tap(Etap, [BEt], 2048)
                p.op("dve", R.tensor_copy(out=Etap2, in_=psS), reads=[bS], writes=[BEt2])
                tap(Etap2, [BEt2], 2560)

        Etap = A.alloc((512,)); Etap2 = A.alloc((512,))
        BEt, BEt2 = Buf("Et"), Buf("Et2")
        sz = A.alloc((512,)); ssq = A.alloc((4,)); junk = A.alloc((512,), BF16)
        Bpp = Buf("pp")

        def finish(qg, h, O, bO):
            yt, by = ytiles[qg % 2], Byt[qg % 2]
            for qt in range(4):
                j = 4 * qg + qt
                p.op("dve", R.scalar_tensor_tensor(
                    out=yt[:, qt, h * 64:(h + 1) * 64], in0=xs_f[:, j, h * 64:(h + 1) * 64], scalar=small[:, 2, h:h + 1],
                    in1=O[:, qt, 0:64], op0=ALU.mult, op1=ALU.add), reads=[bO, Bxf, Bsm], writes=[by])
            if h == 0 and qg == 0:
                tap(yt[:, 0, 0:64], [by], 3072)
                tap(yt[:, 1, 0:64], [by], 3136)
            if h != 7:
                return
            for qt in range(4):
                j = 4 * qg + qt
                y = yt[:, qt, :]
                pb = 4 + (qt % 2)
                proj_tm(wz, j, pb, Bw)
                p.op("act", R.activation(out=sz, in_=banks[pb][:, 0:512], func=AF.Silu), reads=[PB[pb]], writes=[Bpp])
                p.op("dve", R.tensor_tensor(out=y, in0=y, in1=sz, op=ALU.mult), reads=[Bpp, by], writes=[by])
                p.op("act", R.activation(out=junk, in_=y, func=AF.Square, accum_out=ssq[:, qt:qt + 1]),
                     reads=[by], writes=[Bpp])
                p.op("act", R.activation(out=ssq[:, qt:qt + 1], in_=ssq[:, qt:qt + 1], func=AF.Sqrt, bias=eps_t, scale=1.0 / 512),
                     reads=[Bpp, B_const], writes=[Bpp])

                chain("dve", [
                    R.reciprocal(out=ssq[:, qt:qt + 1], in_=ssq[:, qt:qt + 1]),
                    R.scalar_tensor_tensor(out=y, in0=y, scalar=ssq[:, qt:qt + 1], in1=gain_bc, op0=ALU.mult, op1=ALU.mult),
                ], [Bgn], [Bpp, by])
            store_y(0, qg, yt, by)

        BT3 = BT.unsqueeze(1)
        CT3 = CT.unsqueeze(1)

        class _G:
            pass
        def core_ssd():
            Es = [A.alloc((512,), BF16) for _ in range(3)]
            BE = [Buf("E%d" % i) for i in range(3)]
            cnt = 0
            ocnt = 0
            for qg in range(4):
                for h in range(8):
                    gp = (h // 4) * 64
                    ob = 2 + (ocnt % 2)
                    ocnt += 1
                    O = banks[ob][:, 0:4 * 65].rearrange("p (q v) -> p q v", q=4)
                    started = [False] * 4
                    for kt in range(4 * qg + 4):
                        sb = cnt % 2
                        E, bE = Es[cnt % 3], BE[cnt % 3]
                        cnt += 1
                        p.op("pe", R.matmul(
                            banks[sb][:, 0:512], lhsT=BT[gp:gp + 64, kt * 128:(kt + 1) * 128],
                            rhs=CT[gp:gp + 64, qg * 512:(qg + 1) * 512], start=True, stop=True),
                            reads=[BBC], writes=[PB[sb]])
                        make_E(h, qg, kt, banks[sb][:, 0:512], PB[sb], E, bE)
                        for qt in range(4):
                            if kt > 4 * qg + qt:
                                continue
                            p.op("pe", R.matmul(
                                O[:, qt, 0:64], lhsT=E[:, qt * 128:(qt + 1) * 128], rhs=Vt[:, kt, h, 0:64],
                                start=(not any(started)), stop=(kt == 4 * qg + qt), skip_group_check=True),
                                reads=[bE, BV], writes=[PB[ob]])
                            started[qt] = True
                    finish(qg, h, O, PB[ob])
        core_ssd()

    def merge_phase(L):
        p.barrier()
        A.reset(PERSIST)
        Bw = Buf("w")
        wb = [A.alloc((4, D), BF16) for _ in range(4)]
        wg = A.alloc((8, 4 * D), BF16)
        wo = A.alloc((8, D), BF16)
        for i in range(4):
            p.op("pool", R.dma_start(out=wb[i], in_=wbr_d[L, i].rearrange("(c p) n -> p c n", p=128)),
                 writes=[Bw], dma=True)
        for i in range(4):
            p.op("pool", wload(wg[:, :, i * D:(i + 1) * D], w_in_d[L, :, C_G + i * D:C_G + (i + 1) * D]), writes=[Bw], dma=True)
        p.op("pool", wload(wo, wout_d[L]), writes=[Bw], dma=True)
        yin = [A.alloc((4, 512)) for _ in range(2)]
        Byin = [Buf("yin%d" % i) for i in range(2)]
        yb = A.alloc((4, 512), BF16)
        Byb = Buf("yb")
        yT = A.alloc((16, 128), BF16)
        ByT = Buf("yT")
        mg = A.alloc((D,)); sgm = A.alloc((512,)); mgb = A.alloc((D,), BF16); mT = A.alloc((8, 128), BF16)
        Bmg, Bsg, Bmgb, BmT = Buf("mg"), Buf("sg"), Buf("mgb"), Buf("mT")
        hts = [A.alloc((D,)) for _ in range(2)]
        Bht = [Buf("ht%d" % i) for i in range(2)]
        for j in range(NT):
            rows = slice(j * 128, (j + 1) * 128)
            yi, byi = yin[j % 2], Byin[j % 2]
            ht, bh = hts[j % 2], Bht[j % 2]
            p.op("sp", R.dma_start(out=yi, in_=ymix[:, rows, :].rearrange("i p c -> p i c")),
                 reads=B_ymix, writes=[byi], dma=True)
            p.op("sp", R.dma_start(out=ht, in_=hbuf[rows, :]), reads=[B_h], writes=[bh], dma=True)
            p.op("act", R.activation(out=yb, in_=yi, func=AF.Copy), reads=[byi], writes=[Byb])
            for hf in range(2):
                pb = 6 + hf
                pst = bank_bf(pb)
                for q in range(8):
                    ic = hf * 8 + q
                    p.op("pe", R.transpose(
                        out=pst[:, q * 128:(q + 1) * 128], in_=yb[:, ic // 4, (ic % 4) * 128:(ic % 4 + 1) * 128], identity=ident_b),
                        reads=[Byb, B_const], writes=[PB[pb]])
                p.op("act", R.activation(out=yT[:, hf * 8:(hf + 1) * 8, :],
                                                                   in_=pst.rearrange("p (c t) -> p c t", c=8), func=AF.Copy),
                     reads=[PB[pb]], writes=[ByT])
            for i in range(4):
                for hf in range(2):
                    cs = slice(hf * 512, (hf + 1) * 512)
                    for c in range(4):
                        p.op("pe", R.matmul(banks[0 + hf][:, 0:512], lhsT=yT[:, i * 4 + c, :], rhs=wb[i][:, c, cs],
                                                                       start=(c == 0), stop=(c == 3)),
                             reads=[ByT, Bw], writes=[PB[0 + hf]])
                    for kc in range(8):
                        p.op("pe", R.matmul(
                            banks[2 + hf][:, 0:512], lhsT=hnT[:, kc, j * 128:(j + 1) * 128],
                            rhs=wg[:, kc, i * D + hf * 512:i * D + (hf + 1) * 512], start=(kc == 0), stop=(kc == 7)),
                            reads=[B_hnT, Bw], writes=[PB[2 + hf]])
                    p.op("act", R.activation(out=sgm, in_=banks[2 + hf][:, 0:512], func=AF.Sigmoid),
                         reads=[PB[2 + hf]], writes=[Bsg])
                    if i == 0:
                        p.op("dve", R.tensor_tensor(out=mg[:, cs], in0=banks[0 + hf][:, 0:512], in1=sgm, op=ALU.mult),
                             reads=[PB[0 + hf], Bsg], writes=[Bmg])
                    else:
                        chain("dve", [
                            R.tensor_tensor(out=sgm, in0=banks[0 + hf][:, 0:512], in1=sgm, op=ALU.mult),
                            R.tensor_tensor(out=mg[:, cs], in0=mg[:, cs], in1=sgm, op=ALU.add),
                        ], [PB[0 + hf]], [Bsg, Bmg])
            resid_out(mg, Bmg, mgb, Bmgb, mT, BmT, wo, Bw, ht, bh, rows)

    def resid_out(src_f, Bsrc, srcb, Bsrcb, sT, BsT, wo, Bw, ht, bh, rows):
        p.op("act", R.activation(out=srcb, in_=src_f, func=AF.Copy), reads=[Bsrc], writes=[Bsrcb])
        pst = bank_bf(6)
        for c in range(8):
            p.op("pe", R.transpose(out=pst[:, c * 128:(c + 1) * 128], in_=srcb[:, c * 128:(c + 1) * 128], identity=ident_b),
                 reads=[Bsrcb, B_const], writes=[PB[6]])
        p.op("act", R.activation(out=sT, in_=pst.rearrange("p (c t) -> p c t", c=8), func=AF.Copy),
             reads=[PB[6]], writes=[BsT])
        for hf in range(2):
            cs = slice(hf * 512, (hf + 1) * 512)
            for kc in range(8):
                p.op("pe", R.matmul(banks[4 + hf][:, 0:512], lhsT=sT[:, kc, :], rhs=wo[:, kc, cs],
                                                                   start=(kc == 0), stop=(kc == 7)),
                     reads=[BsT, Bw], writes=[PB[4 + hf]])
            p.op("dve", R.tensor_tensor(out=ht[:, cs], in0=ht[:, cs], in1=banks[4 + hf][:, 0:512], op=ALU.add),
                 reads=[PB[4 + hf], bh], writes=[bh])
        p.op("sp", R.dma_start(out=hbuf[rows, :], in_=ht), reads=[bh], writes=[B_h], dma=True)

    def cross_phase(L):
        p.barrier()
        A.reset(PERSIST)
        Bw = Buf("w")
        wq = A.alloc((8, D), BF16); wkv = A.alloc((8, 2 * D), BF16); wo = A.alloc((8, D), BF16)
        p.op("pool", wload(wq, wxq_d[L]), writes=[Bw], dma=True)
        p.op("pool", wload(wkv, wxkv_d[L]), writes=[Bw], dma=True)
        p.op("pool", wload(wo, wxo_d[L]), writes=[Bw], dma=True)
        memb = A.alloc((2, D), BF16)
        Bmem = Buf("mem")
        p.op("pool", R.dma_start(out=memb, in_=mem_d.rearrange("(m p) d -> p m d", p=128)), writes=[Bmem], dma=True)
        memT = A.alloc((8, 256), BF16)
        BmemT = Buf("memT")
        for m in range(2):
            pst = bank_bf(6 + m)
            for c in range(8):
                p.op("pe", R.transpose(out=pst[:, c * 128:(c + 1) * 128], in_=memb[:, m, c * 128:(c + 1) * 128],
                                                                   identity=ident_b), reads=[Bmem, B_const], writes=[PB[6 + m]])
            p.op("act", R.activation(out=memT[:, :, m * 128:(m + 1) * 128],
                                                             in_=pst.rearrange("p (c t) -> p c t", c=8), func=AF.Copy),
                 reads=[PB[6 + m]], writes=[BmemT])
        KxT = A.alloc((8, 256), BF16)
        Vx = A.alloc((2, D), BF16)
        BKx, BVx = Buf("KxT"), Buf("Vx")
        for c in range(8):
            pb = 4 + (c % 2)
            for kc in range(8):
                p.op("pe", R.matmul(banks[pb][:, 0:256], lhsT=wkv[:, kc, c * 128:(c + 1) * 128], rhs=memT[:, kc, :],
                                                              start=(kc == 0), stop=(kc == 7)), reads=[Bw, BmemT], writes=[PB[pb]])
            p.op("act", R.activation(out=KxT[:, c, :], in_=banks[pb][:, 0:256], func=AF.Copy),
                 reads=[PB[pb]], writes=[BKx])
        for m in range(2):
            for hf in range(2):
                pb = 4 + hf
                for kc in range(8):
                    p.op("pe", R.matmul(
                        banks[pb][:, 0:512], lhsT=memT[:, kc, m * 128:(m + 1) * 128], rhs=wkv[:, kc, D + hf * 512:D + (hf + 1) * 512],
                        start=(kc == 0), stop=(kc == 7)), reads=[Bw, BmemT], writes=[PB[pb]])
                p.op("act", R.activation(out=Vx[:, m, hf * 512:(hf + 1) * 512], in_=banks[pb][:, 0:512], func=AF.Copy),
                     reads=[PB[pb]], writes=[BVx])
        QxT = A.alloc((8, S), BF16)
        BQx = Buf("QxT")
        for c in range(8):
            for tg in range(4):
                pb = 4 + (tg % 2)
                proj_fm(wq[:, :, c * 128:(c + 1) * 128], tg, pb, Bw)
                p.op("act", R.activation(out=QxT[:, c, tg * 512:(tg + 1) * 512], in_=banks[pb][:, 0:512], func=AF.Copy),
                     reads=[PB[pb]], writes=[BQx])
        Es = [A.alloc((512,), BF16) for _ in range(3)]
        BE = [Buf("E%d" % i) for i in range(3)]
        xo = A.alloc((4, D))
        Bxo = Buf("xo")
        rc = A.alloc((4,))
        Brc = Buf("rc")
        xob = A.alloc((D,), BF16); xT = A.alloc((8, 128), BF16)
        Bxob, BxT = Buf("xob"), Buf("xT")
        hts = [A.alloc((D,)) for _ in range(2)]
        Bht = [Buf("ht%d" % i) for i in range(2)]
        cnt = 0
        for qg in range(4):
            for hx in range(4):
                Oa = banks[2][:, 0:512].rearrange("p (q v) -> p q v", q=2)
                Ob = banks[3][:, 0:512].rearrange("p (q v) -> p q v", q=2)
                sm = banks[7][:, 0:4]
                for mt in range(2):
                    sb = cnt % 2
                    E, bE = Es[cnt % 3], BE[cnt % 3]
                    cnt += 1
                    for cc in range(2):
                        p.op("pe", R.matmul(
                            banks[sb][:, 0:512], lhsT=KxT[:, 2 * hx + cc, mt * 128:(mt + 1) * 128],
                            rhs=QxT[:, 2 * hx + cc, qg * 512:(qg + 1) * 512], start=(cc == 0), stop=(cc == 1)),
                            reads=[BKx, BQx], writes=[PB[sb]])
                    p.op("act", R.activation(out=E, in_=banks[sb][:, 0:512], func=AF.Exp, scale=1.0 / 16),
                         reads=[PB[sb]], writes=[bE])
                    for qt in range(4):
                        Ot, ob = (Oa, 2) if qt < 2 else (Ob, 3)
                        p.op("pe", R.matmul(
                            Ot[:, qt % 2, :], lhsT=E[:, qt * 128:(qt + 1) * 128], rhs=Vx[:, mt, hx * 256:(hx + 1) * 256],
                            start=(mt == 0 and qt % 2 == 0), stop=(mt == 1), skip_group_check=True), reads=[bE, BVx], writes=[PB[ob]])
                        p.op("pe", R.matmul(
                            sm[:, qt:qt + 1], lhsT=E[:, qt * 128:(qt + 1) * 128], rhs=ones_bf[:, 0:1],
                            start=(mt == 0 and qt == 0), stop=(mt == 1), skip_group_check=True), reads=[bE, B_const], writes=[PB[7]])
                p.op("dve", R.reciprocal(out=rc, in_=banks[7][:, 0:4]), reads=[PB[7]], writes=[Brc])
                for qt in range(4):
                    Ot, ob = (Oa, 2) if qt < 2 else (Ob, 3)
                    p.op("act", R.activation(
                        out=xo[:, qt, hx * 256:(hx + 1) * 256], in_=Ot[:, qt % 2, :], func=AF.Copy, scale=rc[:, qt:qt + 1]),
                        reads=[PB[ob], Brc], writes=[Bxo])
            for qt in range(4):
                j = 4 * qg + qt
                rows = slice(j * 128, (j + 1) * 128)
                ht, bh = hts[j % 2], Bht[j % 2]
                p.op("sp", R.dma_start(out=ht, in_=hbuf[rows, :]), reads=[B_h], writes=[bh], dma=True)
                resid_out(xo[:, qt, :], Bxo, xob, Bxob, xT, BxT, wo, Bw, ht, bh, rows)

    ones_bf = A.alloc((8,), BF16)
    p.op("dve", R.memset(ones_bf, 1.0), writes=[B_const])
    PERSIST = A.mark()

    def peer_phase(L):
        p.barrier()
        A.reset(PERSIST)
        m0 = A.mark()
        AT = A.alloc((S,)); BTt = A.alloc((S,)); WT = A.alloc((S,))
        BABW = Buf("ABW")
        m1 = A.mark()
        Bw = Buf("w")
        wpq = A.alloc((8, 2048), BF16)
        p.op("pool", wload(wpq, wpq_d[L]), writes=[Bw], dma=True)
        skT = A.alloc((16, 128), BF16)
        p.op("pool", R.dma_start(out=skT, in_=skT_d[L].rearrange("c q k -> q c k")), writes=[Bw], dma=True)
        QpT = A.alloc((16, 128), BF16)
        BQp = Buf("QpT")
        ssb = A.alloc((16, 128))
        Bs = Buf("ssb")
        v1 = A.alloc((16,)); v2 = A.alloc((16,)); i1u = A.alloc((16,), U32); i2u = A.alloc((16,), U32)
        i1f = A.alloc((16,)); i2f = A.alloc((16,))
        scr = A.alloc((256,)); cand = A.alloc((256,))
        best = A.alloc((24,)); posu = A.alloc((16,), U32); posf = A.alloc((16,))
        j1f = A.alloc((16,)); j2f = A.alloc((16,)); t3 = A.alloc((16, 16)); t3b = A.alloc((16, 16))
        wex = A.alloc((16,)); zs = A.alloc((2,))
        Btk = Buf("tk")
        Aidx = A.alloc((128,)); Bidx = A.alloc((128,)); Wgt = A.alloc((128,))
        Babw = Buf("abw_tm")
        for j in range(NT):
            for c in range(16):
                pb = c // 4
                for kc in range(8):
                    p.op("pe", R.matmul(
                        banks[pb][:, (c % 4) * 128:(c % 4 + 1) * 128], lhsT=wpq[:, kc, c * 128:(c + 1) * 128],
                        rhs=hnT[:, kc, j * 128:(j + 1) * 128], start=(kc == 0), stop=(kc == 7), skip_group_check=True),
                        reads=[Bw, B_hnT], writes=[PB[pb]])
            for pb in range(4):
                p.op("act", R.activation(out=QpT[:, pb * 4:(pb + 1) * 4, :],
                                                          in_=banks[pb][:, 0:512].rearrange("p (c t) -> p c t", c=4), func=AF.Copy),
                     reads=[PB[pb]], writes=[BQp])
            for c in range(16):
                pb = 4 + c // 4
                p.op("pe", R.matmul(banks[pb][:, (c % 4) * 128:(c % 4 + 1) * 128], lhsT=QpT[:, c, :], rhs=skT[:, c, :],
                                                        start=True, stop=True, skip_group_check=True),
                     reads=[BQp, Bw], writes=[PB[pb]])
            for pb in range(4):
                p.op("act", R.activation(out=ssb[:, pb * 4:(pb + 1) * 4, :],
                                                          in_=banks[4 + pb][:, 0:512].rearrange("p (c t) -> p c t", c=4), func=AF.Copy),
                     reads=[PB[4 + pb]], writes=[Bs])
            for h in range(8):
                fns = []
                for half, (vv, iu) in enumerate(((v1, i1u), (v2, i2u))):
                    sc = ssb[:, 2 * h + half, :]
                    fns.append(R.max(out=vv[:, 0:8], in_=sc))
                    fns.append(R.max_index(out=iu[:, 0:8], in_max=vv[:, 0:8], in_values=sc))
                    fns.append(R.match_replace(out=scr[:, 0:128], in_to_replace=vv[:, 0:8], in_values=sc, imm_value=NEG))
                    fns.append(R.max(out=vv[:, 8:16], in_=scr[:, 0:128]))
                    fns.append(R.max_index(out=iu[:, 8:16], in_max=vv[:, 8:16], in_values=scr[:, 0:128]))
                c3 = cand.rearrange("p (a b) -> p a b", a=16)
                fns += [
                    R.tensor_copy(out=i1f, in_=i1u),
                    R.tensor_copy(out=i2f, in_=i2u),
                    R.tensor_tensor(out=c3, in0=v1.unsqueeze(2).to_broadcast([128, 16, 16]),
                                                     in1=v2.unsqueeze(1).to_broadcast([128, 16, 16]), op=ALU.add),
                    R.max(out=best[:, 0:8], in_=cand),
                    R.max_index(out=posu[:, 0:8], in_max=best[:, 0:8], in_values=cand),
                    R.match_replace(out=scr, in_to_replace=best[:, 0:8], in_values=cand, imm_value=NEG),
                    R.max(out=best[:, 8:16], in_=scr),
                    R.max_index(out=posu[:, 8:16], in_max=best[:, 8:16], in_values=scr),
                    R.tensor_copy(out=posf, in_=posu),
                    R.tensor_tensor(out=t3[:, :, 0:15], in0=posf.unsqueeze(2).to_broadcast([128, 16, 15]),
                                              in1=thr15.unsqueeze(1).to_broadcast([128, 16, 15]), op=ALU.is_ge),
                    R.tensor_reduce(out=j1f, in_=t3[:, :, 0:15], axis=AX.X, op=ALU.add),
                    R.scalar_tensor_tensor(out=j2f, in0=j1f, scalar=-16.0, in1=posf, op0=ALU.mult, op1=ALU.add),
                ]
                for jf, idf, dst in ((j1f, i1f, Aidx), (j2f, i2f, Bidx)):
                    fns += [
                        R.tensor_tensor(out=t3, in0=jf.unsqueeze(2).to_broadcast([128, 16, 16]),
                                                         in1=iota16.unsqueeze(1).to_broadcast([128, 16, 16]), op=ALU.is_equal),
                        R.tensor_tensor(out=t3b, in0=t3, in1=idf.unsqueeze(1).to_broadcast([128, 16, 16]), op=ALU.mult),
                        R.tensor_reduce(out=dst[:, h * 16:(h + 1) * 16], in_=t3b, axis=AX.X, op=ALU.add),
                    ]
                fns.append(R.tensor_scalar(out=zs[:, 0:1], in0=best[:, 0:1], scalar1=-1.0, scalar2=None, op0=ALU.mult))
                chain("dve", fns, [Bs, B_const], [Btk, Babw])
                p.op("act", R.activation(out=wex, in_=best[:, 0:16], func=AF.Exp, bias=zs[:, 0:1], scale=1.0,
                                                   accum_out=zs[:, 1:2]), reads=[Btk], writes=[Btk])
                chain("dve", [
                    R.reciprocal(out=zs[:, 1:2], in_=zs[:, 1:2]),
                    R.tensor_scalar(out=Wgt[:, h * 16:(h + 1) * 16], in0=wex, scalar1=zs[:, 1:2], scalar2=None, op0=ALU.mult),
                ], [], [Btk, Babw])
            for q, (srcv, dstv) in enumerate(((Aidx, AT), (Bidx, BTt), (Wgt, WT))):
                p.op("pe", R.transpose(out=banks[q][:, 0:128], in_=srcv, identity=ident_f),
                     reads=[Babw, B_const], writes=[PB[q]])
                p.op("act", R.activation(out=dstv[:, j * 128:(j + 1) * 128], in_=banks[q][:, 0:128], func=AF.Copy),
                     reads=[PB[q]], writes=[BABW])

        p.barrier()
        A.reset(m1)
        TG = 256
        wbig = [A.alloc((128, TG), BF16) for _ in range(2)]
        Bwb = [Buf("wbig%d" % i) for i in range(2)]
        Pm = [A.alloc((128,), BF16) for _ in range(4)]
        Qm = [A.alloc((128,), BF16) for _ in range(4)]
        BPQ = [Buf("PQ%d" % i) for i in range(4)]
        B_wd = Buf("wd")
        for tg in range(S // TG):
            wbg, bwb = wbig[tg % 2], Bwb[tg % 2]
            for t4 in range(TG // 4):
                pb = t4 % 4
                for u in range(4):
                    t = tg * TG + t4 * 4 + u
                    s4 = (t4 * 4 + u) % 4
                    Pt, Qt, bpq = Pm[s4], Qm[s4], BPQ[s4]

                    def mk(e, t=t, Pt=Pt, Qt=Qt):
                        e.tensor_scalar(out=Pt, in0=iota_row, scalar1=AT[:, t:t + 1], scalar2=None, op0=ALU.is_equal)
                        return e.tensor_scalar(out=Qt, in0=iota_row, scalar1=BTt[:, t:t + 1], scalar2=WT[:, t:t + 1],
                                               op0=ALU.is_equal, op1=ALU.mult)
                    p.op("dve", mk, reads=[BABW, B_const], writes=[bpq])
                    p.op("pe", R.matmul(banks[pb][:, u * 128:(u + 1) * 128], lhsT=Qt, rhs=Pt,
                                                                           start=True, stop=True, skip_group_check=True),
                         reads=[bpq], writes=[PB[pb]])
                p.op("act", R.activation(
                    out=wbg[:, :, t4 * 4:(t4 + 1) * 4], in_=banks[pb][:, 0:512].rearrange("p (u i) -> p i u", u=4), func=AF.Copy),
                    reads=[PB[pb]], writes=[bwb])
            for q4 in range(4):
                p.op("sp", R.dma_start(
                    out=wd[q4 * 32:(q4 + 1) * 32, :, tg * TG:(tg + 1) * TG].rearrange("a p t -> p a t"),
                    in_=wbg[:, q4 * 32:(q4 + 1) * 32, :]),
                    reads=[bwb], writes=[B_wd], dma=True)

        p.barrier()
        A.reset(m0)
        yacc = A.alloc((8, S))
        Byacc = Buf("yacc")
        NB = 4
        wblk = [A.alloc((NB, S), BF16) for _ in range(2)]
        ublk = [A.alloc((8, NB * 128), BF16) for _ in range(2)]
        vblk = [A.alloc((NB, D), BF16) for _ in range(2)]
        Bblk = [Buf("blk%d" % i) for i in range(2)]
        Gs = [A.alloc((512,), BF16) for _ in range(3)]
        BG = [Buf("G%d" % i) for i in range(3)]
        GWs = [A.alloc((NB, 512), BF16) for _ in range(2)]
        BGW = [Buf("GW%d" % i) for i in range(2)]
        gcnt = 0
        it = 0
        ycnt = 0
        for eb in range(128 // NB):
            sl = eb % 2
            wb_, ub_, vb_, bb = wblk[sl], ublk[sl], vblk[sl], Bblk[sl]
            p.op("sp", R.dma_start(out=wb_, in_=wd[eb * NB:(eb + 1) * NB].rearrange("a p t -> p a t")),
                 reads=[B_wd], writes=[bb], dma=True)
            p.op("pool", wload(ub_, uT_d[L, :, eb * NB * 128:(eb + 1) * NB * 128]), writes=[bb], dma=True)
            p.op("pool", R.dma_start(
                out=vb_, in_=pv_d[L, eb * NB * 128:(eb + 1) * NB * 128, :].rearrange("(a p) d -> p a d", p=128)),
                writes=[bb], dma=True)
            for tg in range(4):
                gw, bgw = GWs[it % 2], BGW[it % 2]
                it += 1
                for a in range(NB):
                    sb = gcnt % 2
                    G, bG = Gs[gcnt % 3], BG[gcnt % 3]
                    gcnt += 1
                    for kc in range(8):
                        p.op("pe", R.matmul(
                            banks[sb][:, 0:512], lhsT=ub_[:, kc, a * 128:(a + 1) * 128], rhs=hnT[:, kc, tg * 512:(tg + 1) * 512],
                            start=(kc == 0), stop=(kc == 7)), reads=[bb, B_hnT], writes=[PB[sb]])
                    p.op("act", R.activation(out=G, in_=banks[sb][:, 0:512], func=AF.Gelu),
                         reads=[PB[sb]], writes=[bG])
                    p.op("dve", R.tensor_tensor(
                        out=gw[:, a, :], in0=G, in1=wb_[:, a, tg * 512:(tg + 1) * 512], op=ALU.mult),
                        reads=[bG, bb], writes=[bgw])
                for dc in range(8):
                    yb_ = 2 + (ycnt % 4)
                    ycnt += 1
                    for a in range(NB):
                        p.op("pe", R.matmul(
                            banks[yb_][:, 0:512], lhsT=vb_[:, a, dc * 128:(dc + 1) * 128], rhs=gw[:, a, :],
                            start=(a == 0), stop=(a == NB - 1)), reads=[bb, bgw], writes=[PB[yb_]])
                    ysl = yacc[:, dc, tg * 512:(tg + 1) * 512]
                    if eb == 0:
                        p.op("dve", R.tensor_copy(out=ysl, in_=banks[yb_][:, 0:512]),
                             reads=[PB[yb_]], writes=[Byacc])
                    else:
                        p.op("dve", R.tensor_tensor(out=ysl, in0=ysl, in1=banks[yb_][:, 0:512], op=ALU.add),
                             reads=[PB[yb_], Byacc], writes=[Byacc])

        p.barrier()
        hts = [A.alloc((D,)) for _ in range(2)]
        Bht = [Buf("ht%d" % i) for i in range(2)]
        for j in range(NT):
            rows = slice(j * 128, (j + 1) * 128)
            ht, bh = hts[j % 2], Bht[j % 2]
            p.op("sp", R.dma_start(out=ht, in_=hbuf[rows, :]), reads=[B_h], writes=[bh], dma=True)
            for hf in range(2):
                pb = 4 + hf
                for c in range(4):
                    dc = hf * 4 + c
                    p.op("pe", R.transpose(
                        out=banks[pb][:, c * 128:(c + 1) * 128], in_=yacc[:, dc, j * 128:(j + 1) * 128], identity=ident_f),
                        reads=[Byacc, B_const], writes=[PB[pb]])
                p.op("dve", R.tensor_tensor(
                    out=ht[:, hf * 512:(hf + 1) * 512], in0=ht[:, hf * 512:(hf + 1) * 512], in1=banks[pb][:, 0:512], op=ALU.add),
                    reads=[PB[pb], bh], writes=[bh])
            p.op("sp", R.dma_start(out=hbuf[rows, :], in_=ht), reads=[bh], writes=[B_h], dma=True)

    stages = dbg["stages"] if dbg else None

    def want(name):
        return stages is None or name in stages

    for L in range(DEPTH):
        if dbg and L >= dbg.get("layers", DEPTH):
            break
        norm_phase(mixn_d[L:L + 1, :], src=(x_d if L == 0 else None))
        if want("ssd"):
            ssd_mixer(L)
        if want("ret"):
            ret_mixer(L)
        if want("moba"):
            softmax_mixer(L, 2)
        if want("dil"):
            softmax_mixer(L, 3)
        if want("merge"):
            merge_phase(L)
        if want("cross"):
            norm_phase(xn_d[L:L + 1, :])
            cross_phase(L)
        if want("peer"):
            norm_phase(fn_d[L:L + 1, :])
            peer_phase(L)
    norm_phase(fnorm_d[0:1, :], final_out=out_d)
    fin = p.op("sp", None)
    fin.deps = list(p.all_dma[-40:])
    p.emit(st)
    return nc, st


_CACHE = {}


def _prep_weights(inp):
    w_in = np.ascontiguousarray(inp["w_in"], dtype=np.float32)
    sw = []
    for c0 in (C_RQ, C_RK, C_MQ, C_MK, C_DQ, C_DK):
        sw.append(np.stack([_swap_halves(w_in[L, :, c0:c0 + 512]) for L in range(DEPTH)], 0))
    w_sw = np.ascontiguousarray(np.concatenate(sw, axis=2))
    sk = np.asarray(inp["peer_sub_keys"], dtype=np.float32)
    skT = np.ascontiguousarray(sk.reshape(DEPTH, 16, 128, 128).transpose(0, 1, 3, 2))
    uT = np.ascontiguousarray(np.asarray(inp["peer_u"], dtype=np.float32).transpose(0, 2, 1))
    shared = {
        "w_in": w_in, "w_sw": w_sw, "skT": skT, "peer_uT": uT,
        "peer_v": np.ascontiguousarray(inp["peer_v"], dtype=np.float32),
        "final_norm": np.ascontiguousarray(inp["final_norm"], dtype=np.float32).reshape(1, D),
    }
    for k in ("ssd_conv_w", "ssd_conv_b", "ssd_dt_bias", "ssd_a_log", "ssd_d", "ssd_norm", "ret_norm", "w_branch",
              "w_out", "mix_norm", "x_norm", "w_xq", "w_xkv", "w_xo", "ffn_norm", "w_pq"):
        shared[k] = np.ascontiguousarray(inp[k], dtype=np.float32)
    for k, v in _consts().items():
        shared["c_" + k] = v
    return shared


def kernel(**inputs):
    x = np.asarray(inputs["x"], dtype=np.float32)
    mem = np.asarray(inputs["mem"], dtype=np.float32)
    nb = x.shape[0]
    shared = _prep_weights(inputs)
    if "nc" not in _CACHE:
        _CACHE["nc"] = build()
    nc, _st = _CACHE["nc"]
    in_maps = []
    for b in range(nb):
        m = dict(shared)
        m["x"] = np.ascontiguousarray(x[b])
        m["mem"] = np.ascontiguousarray(mem[b])
        in_maps.append(m)
    res = run_bass_kernel_spmd(nc, in_maps, core_ids=list(range(nb)))
    return np.stack([np.asarray(r["out"], dtype=np.float32) for r in res.results], 0)
```

```python
import numpy as np
from contextlib import ExitStack
import concourse.bass as bass
import concourse.mybir as mybir
from concourse.bass_utils import run_bass_kernel_spmd

F32 = mybir.dt.float32
BF16 = mybir.dt.bfloat16
U32 = mybir.dt.uint32
ALU = mybir.AluOpType
AF = mybir.ActivationFunctionType
AX = mybir.AxisListType

S = 2048
D = 1024
NT = 16
DEPTH = 2
IN_DIM = 10504
EPS = 1e-6
NEG = -1e30
C_Z, C_XBC, C_DT = 0, 512, 1280
C_RQ, C_RK, C_RV, C_RG = 1288, 1800, 2312, 2824
C_MQ, C_MK, C_MV = 3336, 3848, 4360
C_DQ, C_DK, C_DV = 4872, 5384, 5896
C_G = 6408


class Buf:
    __slots__ = ("name", "last_w", "readers", "excl")

    def __init__(self, name, excl=False):
        self.name = name
        self.excl = excl
        self.last_w = None
        self.readers = []


class Op:
    __slots__ = ("eng", "fn", "deps", "dma", "sem", "target", "needs_sig", "sig", "idx")


class Prog:
    ENGS = ("pe", "act", "dve", "pool", "sp")
    NDMA = {"sp": 12, "pool": 12, "act": 6}

    def __init__(self, nc):
        self.nc = nc
        self.ops = {e: [] for e in self.ENGS}
        self.nops = 0
        self.dma_hist = {e: [] for e in self.NDMA}
        self.dma_n = {e: 0 for e in self.NDMA}
        self.dma_cnt = {e: [0] * n for e, n in self.NDMA.items()}
        self.pending_barrier = {e: [] for e in self.ENGS}
        self.all_dma = []

    def op(self, eng, fn, reads=(), writes=(), dma=False):
        o = Op()
        o.eng = eng
        o.fn = fn
        o.dma = dma
        o.needs_sig = False
        o.sig = None
        o.sem = None
        o.target = None
        o.idx = self.nops
        self.nops += 1
        deps = {}
        raw = set()
        for b in reads:
            if b.last_w is not None:
                deps[b.last_w.idx] = b.last_w
                raw.add(b.last_w.idx)
            if b.excl:
                for r in b.readers:
                    deps[r.idx] = r
        for b in writes:
            if b.last_w is not None:
                deps[b.last_w.idx] = b.last_w
            for r in b.readers:
                deps[r.idx] = r
        for d in self.pending_barrier[eng]:
            deps[d.idx] = d
        self.pending_barrier[eng] = []
        if dma:
            k = self.dma_n[eng] % self.NDMA[eng]
            self.dma_n[eng] += 1
            self.dma_cnt[eng][k] += 16
            o.sem = (eng, k)
            o.target = self.dma_cnt[eng][k]
            hist = self.dma_hist[eng]
            if len(hist) >= self.NDMA[eng]:
                prev = hist[-self.NDMA[eng]]
                deps[prev.idx] = prev
            hist.append(o)
            self.all_dma.append(o)
        o.deps = []
        for d in deps.values():
            if d is o:
                continue
            if d.dma or d.eng != eng or (d.idx in raw and eng != "pe"):
                o.deps.append(d)
                d.needs_sig = True
        for b in reads:
            if not dma:
                b.readers = [r for r in b.readers if r.dma or r.eng != eng]
            b.readers.append(o)
        for b in writes:
            b.last_w = o
            b.readers = []
        self.ops[eng].append(o)
        return o

    def barrier(self):
        lasts = []
        for e in self.ENGS:
            for o in reversed(self.ops[e]):
                if not o.dma and o.fn is not None:
                    lasts.append(o)
                    break
        lasts.extend(self.all_dma[-36:])
        for e in self.ENGS:
            self.pending_barrier[e] = list(lasts)

    def emit(self, stack):
        nc = self.nc
        esem = {e: stack.enter_context(nc.semaphore("s_" + e)) for e in self.ENGS}
        dsem = {}
        for e, n in self.NDMA.items():
            for k in range(n):
                dsem[(e, k)] = stack.enter_context(nc.semaphore("d_%s%d" % (e, k)))
        for e in self.ENGS:
            c = 0
            for o in self.ops[e]:
                if o.needs_sig and not o.dma:
                    c += 1
                    o.sig = c
        block = stack.enter_context(nc.Block())

        def run(name):
            def f(eng):
                seen = {}
                for o in self.ops[name]:
                    for d in o.deps:
                        if d.dma:
                            key, val, sem = d.sem, d.target, dsem[d.sem]
                        else:
                            key, val, sem = d.eng, d.sig, esem[d.eng]
                        if seen.get(key, 0) >= val:
                            continue
                        seen[key] = val
                        eng.wait_ge(sem, val)
                    if o.fn is None:
                        continue
                    ins = o.fn(eng)
                    if o.dma:
                        ins.then_inc(dsem[o.sem], 16)
                    elif o.needs_sig:
                        ins.then_inc(esem[name], 1)
            return f

        block.tensor(run("pe"))
        block.scalar(run("act"))
        block.vector(run("dve"))
        block.gpsimd(run("pool"))
        block.sync(run("sp"))


class _Rec:
    def __getattr__(self, name):
        def mk(*a, **k):
            return lambda e: getattr(e, name)(*a, **k)
        return mk


R = _Rec()


class Arena:
    def __init__(self, tensor, nfloats):
        self.t = tensor
        self.n = nfloats
        self.base = 0
        self.off = 0

    def alloc(self, shape, dt=F32):
        n = int(np.prod(shape))
        nf = n if dt in (F32, U32) else (n + 1) // 2
        nf = (nf + 7) // 8 * 8
        assert self.off + nf <= self.n, ("SBUF arena overflow", self.off, nf, self.n)
        ap = self.t[:, self.off:self.off + nf]
        self.off += nf
        if dt != F32:
            ap = ap.bitcast(dt)
        ap = ap[:, 0:n]
        if len(shape) == 2:
            ap = ap.rearrange("p (a b) -> p a b", a=shape[0])
        elif len(shape) == 3:
            ap = ap.rearrange("p (a b c) -> p a b c", a=shape[0], b=shape[1])
        return ap

    def mark(self):
        return self.off

    def reset(self, m):
        self.off = m


def _consts():
    c = {}
    c["ident"] = np.eye(128, dtype=np.float32)
    half = 32
    inv_freq = (10000.0 ** (-np.arange(half, dtype=np.float32) / half)).astype(np.float32)
    ang = np.arange(S, dtype=np.float32)[:, None] * inv_freq[None, :]
    cos = np.cos(ang).astype(np.float32).T
    sin = np.sin(ang).astype(np.float32).T
    cos64 = np.concatenate([cos, cos], 0)
    sin64 = np.concatenate([-sin, sin], 0)
    c["ropec"] = np.ascontiguousarray(np.concatenate([cos64, cos64], 0))
    c["ropes"] = np.ascontiguousarray(np.concatenate([sin64, sin64], 0))
    s = np.arange(128)[:, None]
    l = np.arange(512)[None, :]
    c["trim"] = np.stack([(128 * j + s <= l) for j in range(4)], 1).astype(np.float32)
    lg = np.log1p(-np.exp2(-5.0 - np.arange(8, dtype=np.float64)))
    c["rett"] = np.stack([np.exp((l - s) * lg[h]) for h in range(8)], 1).astype(np.float32)
    dm = []
    for dlt in list(range(-3, 5)) + [5]:
        dist = 128 * dlt + l - s
        m = ((dist >= 0) & (dist <= 128)).astype(np.float32)
        m += ((dist >= 0) & (dist <= 512) & (dist % 4 == 0)).astype(np.float32)
        m += ((dist >= 0) & (dist <= 2048) & (dist % 16 == 0)).astype(np.float32)
        dm.append(m)
    c["dilm"] = np.stack(dm, 1).astype(np.float32)
    x = np.arange(3968)[None, :]
    c["cumm"] = (x >= s + 1920).astype(np.float32)
    sel = np.zeros((8, 8, 128), np.float32)
    for h in range(8):
        sel[h, h, :] = 1.0
    c["hsel"] = sel.transpose(1, 0, 2).copy()
    misc = np.zeros((128, 256), np.float32)
    misc[:, 0:128] = np.arange(128)[None, :]
    misc[:, 128:143] = 16.0 * np.arange(1, 16)[None, :]
    misc[:, 144:160] = np.arange(16)[None, :]
    misc[:, 160] = EPS
    misc[:, 161] = 1.0
    c["misc"] = misc
    return c


def _swap_halves(w):
    w = w.reshape(w.shape[0], 8, 2, 32)
    return np.ascontiguousarray(w[:, :, ::-1, :].reshape(w.shape[0], 512))


def build(dbg=None):
    nc = bass.Bass("TRN2", target_bir_lowering=False)

    def din(name, shape, dt=F32):
        return nc.dram_tensor(name, list(shape), dt, kind="ExternalInput").ap()

    x_d = din("x", [S, D])
    mem_d = din("mem", [256, D])
    w_in_d = din("w_in", [DEPTH, D, IN_DIM])
    w_sw_d = din("w_sw", [DEPTH, D, 6 * 512])
    convw_d = din("ssd_conv_w", [DEPTH, 4, 768])
    convb_d = din("ssd_conv_b", [DEPTH, 768])
    dtb_d = din("ssd_dt_bias", [DEPTH, 8])
    alog_d = din("ssd_a_log", [DEPTH, 8])
    sd_d = din("ssd_d", [DEPTH, 8])
    snorm_d = din("ssd_norm", [DEPTH, 512])
    rnorm_d = din("ret_norm", [DEPTH, 512])
    wbr_d = din("w_branch", [DEPTH, 4, 512, D])
    wout_d = din("w_out", [DEPTH, D, D])
    mixn_d = din("mix_norm", [DEPTH, D])
    xn_d = din("x_norm", [DEPTH, D])
    wxq_d = din("w_xq", [DEPTH, D, D])
    wxkv_d = din("w_xkv", [DEPTH, D, 2 * D])
    wxo_d = din("w_xo", [DEPTH, D, D])
    fn_d = din("ffn_norm", [DEPTH, D])
    wpq_d = din("w_pq", [DEPTH, D, 2048])
    skT_d = din("skT", [DEPTH, 16, 128, 128])
    uT_d = din("peer_uT", [DEPTH, D, 16384])
    pv_d = din("peer_v", [DEPTH, 16384, D])
    fnorm_d = din("final_norm", [1, D])
    cd = {k: din("c_" + k, v.shape) for k, v in _consts().items()}

    out_d = nc.dram_tensor("out", [S, D], F32, kind="ExternalOutput").ap()
    okind = "ExternalOutput" if dbg else "Internal"
    hbuf = nc.dram_tensor("hbuf", [S, D], F32, kind=okind).ap()
    ymix = nc.dram_tensor("ymix", [4, S, 512], F32, kind=okind).ap()
    wd = nc.dram_tensor("wd", [128, 128, S], BF16, kind="Internal").ap()
    dbg_d = nc.dram_tensor("dbgbuf", [128, 4096], F32, kind=okind).ap()

    st = ExitStack()
    arena_t = st.enter_context(nc.sbuf_tensor("arena", [128, 200 * 256], F32))
    banks = [st.enter_context(nc.psum_tensor("bank%d" % i, [128, 512], F32)) for i in range(8)]
    A = Arena(arena_t, 200 * 256)
    p = Prog(nc)
    PB = [Buf("ps%d" % i, excl=True) for i in range(8)]

    def bank_bf(i):
        return banks[i][:, :].bitcast(BF16)

    hnT = A.alloc((8, S), BF16)
    B_hnT = Buf("hnT")
    ident_f = A.alloc((128,))
    ident_b = A.alloc((128,), BF16)
    misc = A.alloc((256,))
    trim = A.alloc((4, 512), BF16)
    B_const = Buf("const")
    iota_row = misc[:, 0:128]
    thr15 = misc[:, 128:143]
    iota16 = misc[:, 144:160]
    eps_t = misc[:, 160:161]
    ones_t = misc[:, 161:162]
    p.op("sp", R.dma_start(out=ident_f, in_=cd["ident"]), writes=[B_const], dma=True)
    p.op("pool", R.dma_start(out=ident_b, in_=cd["ident"]), writes=[B_const], dma=True)
    p.op("sp", R.dma_start(out=misc, in_=cd["misc"]), writes=[B_const], dma=True)
    p.op("pool", R.dma_start(out=trim, in_=cd["trim"]), writes=[B_const], dma=True)
    PERSIST = A.mark()

    B_h = Buf("hbuf")
    B_ymix = [Buf("ymix%d" % i) for i in range(4)]

    def chain(eng, fns, reads, scratch):
        for f in fns:
            p.op(eng, f, reads=list(reads) + list(scratch), writes=list(scratch))

    B_dbg = Buf("dbg")

    def tap(ap, bufs, col0):
        if not dbg:
            return
        n = ap.shape[1]
        pp = ap.shape[0]
        p.op("sp", R.dma_start(out=dbg_d[0:pp, col0:col0 + n], in_=ap), reads=bufs, writes=[B_dbg], dma=True)

    def wload(dst, src2d, eng="pool"):
        return R.dma_start(out=dst, in_=src2d.rearrange("(kc p) n -> p kc n", p=128))

    def bcast_row(dst, row_ap, bufs, eng="sp"):
        p.op(eng, R.dma_start(out=dst, in_=row_ap.partition_broadcast(128)), writes=bufs, dma=True)

    def norm_phase(gain_row, src=None, final_out=None):
        p.barrier()
        A.reset(PERSIST)
        gain_bc = A.alloc((D,))
        Bg = Buf("gain")
        bcast_row(gain_bc, gain_row, [Bg])
        hts = [A.alloc((D,)) for _ in range(2)]
        Bht = [Buf("ht%d" % i) for i in range(2)]
        junk = A.alloc((D,), BF16)
        hnb = [A.alloc((D,), BF16) for _ in range(2)]
        Bhnb = [Buf("hnb%d" % i) for i in range(2)]
        outs = [A.alloc((D,)) for _ in range(2)]
        Bouts = [Buf("no%d" % i) for i in range(2)]
        ss = A.alloc((NT,))
        rs = A.alloc((NT,))
        Bss = Buf("ss")
        srcd = hbuf if src is None else src
        for j in range(NT):
            ht, bh = hts[j % 2], Bht[j % 2]
            rows = slice(j * 128, (j + 1) * 128)
            p.op("sp", R.dma_start(out=ht, in_=srcd[rows, :]),
                 reads=[B_h], writes=[bh], dma=True)
            if src is not None:
                p.op("sp", R.dma_start(out=hbuf[rows, :], in_=ht),
                     reads=[bh], writes=[B_h], dma=True)
            p.op("act", R.activation(out=junk, in_=ht, func=AF.Square, accum_out=ss[:, j:j + 1]),
                 reads=[bh], writes=[Bss])
            p.op("act", R.activation(out=rs[:, j:j + 1], in_=ss[:, j:j + 1], func=AF.Sqrt,
                                                      bias=eps_t, scale=1.0 / D),
                 reads=[Bss, B_const], writes=[Bss])
            p.op("dve", R.reciprocal(out=rs[:, j:j + 1], in_=rs[:, j:j + 1]), reads=[Bss], writes=[Bss])
            if final_out is not None:
                o, bo = outs[j % 2], Bouts[j % 2]
                p.op("dve", R.scalar_tensor_tensor(
                    out=o, in0=ht, scalar=rs[:, j:j + 1], in1=gain_bc, op0=ALU.mult, op1=ALU.mult),
                    reads=[bh, Bss, Bg], writes=[bo])
                p.op("sp", R.dma_start(out=final_out[rows, :], in_=o),
                     reads=[bo], writes=[B_out], dma=True)
                continue
            hb, bhb = hnb[j % 2], Bhnb[j % 2]
            p.op("dve", R.scalar_tensor_tensor(
                out=hb, in0=ht, scalar=rs[:, j:j + 1], in1=gain_bc, op0=ALU.mult, op1=ALU.mult),
                reads=[bh, Bss, Bg], writes=[bhb])
            pb = 6 + (j % 2)
            pst = bank_bf(pb)
            for c in range(8):
                p.op("pe", R.transpose(
                    out=pst[:, c * 128:(c + 1) * 128], in_=hb[:, c * 128:(c + 1) * 128], identity=ident_b),
                    reads=[bhb, B_const], writes=[PB[pb]])
            p.op("act", R.activation(
                out=hnT[:, :, j * 128:(j + 1) * 128], in_=pst.rearrange("p (c t) -> p c t", c=8), func=AF.Copy),
                reads=[PB[pb]], writes=[B_hnT])

    B_out = Buf("out")

    def proj_fm(w_ap, tg, pb, wbuf, n=512, t0=None):
        t0 = tg * 512 if t0 is None else t0
        for kc in range(8):
            p.op("pe", R.matmul(banks[pb][:w_ap.shape[2], 0:n], lhsT=w_ap[:, kc, :],
                                                  rhs=hnT[:, kc, t0:t0 + n], start=(kc == 0), stop=(kc == 7)),
                 reads=[wbuf, B_hnT], writes=[PB[pb]])

    def proj_tm(w_ap, j, pb, wbuf, n=512):
        for kc in range(8):
            p.op("pe", R.matmul(banks[pb][:, 0:n], lhsT=hnT[:, kc, j * 128:(j + 1) * 128],
                                                  rhs=w_ap[:, kc, 0:n], start=(kc == 0), stop=(kc == 7)),
                 reads=[wbuf, B_hnT], writes=[PB[pb]])

    def attn_core(score_fn, Vt, nv, bV, make_E, finish):
        Es = [A.alloc((512,), BF16) for _ in range(3)]
        BE = [Buf("E%d" % i) for i in range(3)]
        items = [(qg, h, kt) for qg in range(4) for h in range(8) for kt in range(4 * qg + 4)]
        score_fn(items[0][1], items[0][0], items[0][2], 0)
        ocnt = -1
        O = ob = started = None
        for idx, (qg, h, kt) in enumerate(items):
            if kt == 0:
                ocnt += 1
                ob = 2 + (ocnt % 2)
                O = banks[ob][:, 0:4 * 65].rearrange("p (q v) -> p q v", q=4)
                started = [False] * 4
            if idx + 1 < len(items):
                nq, nh, nk = items[idx + 1]
                score_fn(nh, nq, nk, (idx + 1) % 2)
            sb = idx % 2
            E, bE = Es[idx % 3], BE[idx % 3]
            make_E(h, qg, kt, banks[sb][:, 0:512], PB[sb], E, bE)
            for qt in range(4):
                if kt > 4 * qg + qt:
                    continue
                p.op("pe", R.matmul(
                    O[:, qt, 0:nv], lhsT=E[:, qt * 128:(qt + 1) * 128], rhs=Vt[:, kt, h, 0:nv],
                    start=(not any(started)), stop=(kt == 4 * qg + qt), skip_group_check=True),
                    reads=[bE, bV], writes=[PB[ob]])
                started[qt] = True
            if kt == 4 * qg + 3:
                finish(qg, h, O, PB[ob])

    def qk_score(QT, KT, BQ, BK):
        def f(h, qg, kt, sb):
            hp, hc = (h % 2) * 64, h // 2
            p.op("pe", R.matmul(banks[sb][:, 0:512], lhsT=KT[hp:hp + 64, hc, kt * 128:(kt + 1) * 128],
                                rhs=QT[hp:hp + 64, hc, qg * 512:(qg + 1) * 512], start=True, stop=True),
                 reads=[BQ, BK], writes=[PB[sb]])
        return f

    def proj_rope(dst, bdst, w, wsw, bw, ropec, ropes, Brope, tmp1, tmp2, Btmp):
        for c in range(4):
            for tg in range(4):
                proj_fm(w[:, :, c * 128:(c + 1) * 128], tg, 4, bw)
                proj_fm(wsw[:, :, c * 128:(c + 1) * 128], tg, 5, bw)
                sl = slice(tg * 512, (tg + 1) * 512)
                p.op("dve", R.tensor_tensor(out=tmp1, in0=banks[4][:, 0:512], in1=ropec[:, sl], op=ALU.mult),
                     reads=[PB[4], Brope], writes=[Btmp])
                p.op("dve", R.tensor_tensor(out=tmp2, in0=banks[5][:, 0:512], in1=ropes[:, sl], op=ALU.mult),
                     reads=[PB[5], Brope], writes=[Btmp])
                p.op("dve", R.tensor_tensor(out=dst[:, c, sl], in0=tmp1, in1=tmp2, op=ALU.add),
                     reads=[Btmp], writes=[bdst])

    def proj_v(Vt, bV, w, bw, nones):
        if nones:
            p.op("dve", R.memset(Vt[:, :, :, 64:65], 1.0), writes=[bV])
        for j in range(NT):
            pb = 4 + (j % 2)
            proj_tm(w, j, pb, bw)
            p.op("act", R.activation(
                out=Vt[:, j, :, 0:64], in_=banks[pb][:, 0:512].rearrange("p (h v) -> p h v", h=8), func=AF.Copy),
                reads=[PB[pb]], writes=[bV])

    def store_y(i, qg, ytile, by):
        p.op("sp", R.dma_start(
            out=ymix[i, qg * 512:(qg + 1) * 512, :].rearrange("(q p) c -> p q c", p=128), in_=ytile),
            reads=[by], writes=[B_ymix[i]], dma=True)

    def softmax_mixer(L, kind):
        p.barrier()
        A.reset(PERSIST)
        cq, ck, cv = (C_MQ, C_MK, C_MV) if kind == 2 else (C_DQ, C_DK, C_DV)
        swq = (2 if kind == 2 else 4) * 512
        Bw = Buf("w")
        wq = A.alloc((8, 512), BF16); wqs = A.alloc((8, 512), BF16)
        wk = A.alloc((8, 512), BF16); wks = A.alloc((8, 512), BF16)
        wv = A.alloc((8, 512), BF16)
        p.op("pool", wload(wq, w_in_d[L, :, cq:cq + 512]), writes=[Bw], dma=True)
        p.op("pool", wload(wqs, w_sw_d[L, :, swq:swq + 512]), writes=[Bw], dma=True)
        p.op("pool", wload(wk, w_in_d[L, :, ck:ck + 512]), writes=[Bw], dma=True)
        p.op("pool", wload(wks, w_sw_d[L, :, swq + 512:swq + 1024]), writes=[Bw], dma=True)
        p.op("pool", wload(wv, w_in_d[L, :, cv:cv + 512]), writes=[Bw], dma=True)
        ropec = A.alloc((S,)); ropes = A.alloc((S,))
        Brope = Buf("rope")
        p.op("sp", R.dma_start(out=ropec, in_=cd["ropec"]), writes=[Brope], dma=True)
        p.op("sp", R.dma_start(out=ropes, in_=cd["ropes"]), writes=[Brope], dma=True)
        QT = A.alloc((4, S), BF16); KT = A.alloc((4, S), BF16)
        BQ, BK, BV = Buf("QT"), Buf("KT"), Buf("Vt")
        Vt = A.alloc((NT, 8, 65), BF16)
        tmp1 = A.alloc((512,)); tmp2 = A.alloc((512,))
        Btmp = Buf("tmp")
        proj_rope(QT, BQ, wq, wqs, Bw, ropec, ropes, Brope, tmp1, tmp2, Btmp)
        proj_rope(KT, BK, wk, wks, Bw, ropec, ropes, Brope, tmp1, tmp2, Btmp)
        proj_v(Vt, BV, wv, Bw, True)
        ytiles = [A.alloc((4, 512)) for _ in range(2)]
        Byt = [Buf("yt%d" % i) for i in range(2)]
        rc = A.alloc((4,))
        Brc = Buf("rc")
        if kind == 3:
            dilm = A.alloc((9, 512), BF16)
            Bdm = Buf("dilm")
            p.op("pool", R.dma_start(out=dilm, in_=cd["dilm"]), writes=[Bdm], dma=True)

            def make_E(h, qg, kt, psS, bS, E, bE):
                p.op("act", R.activation(out=E, in_=psS, func=AF.Exp, scale=0.125), reads=[bS], writes=[bE])
                dlt = 4 * qg - kt
                mi = 8 if dlt >= 5 else dlt + 3
                p.op("dve", R.tensor_tensor(out=E, in0=E, in1=dilm[:, mi, :], op=ALU.mult),
                     reads=[bE, Bdm], writes=[bE])

            def finish(qg, h, O, bO):
                yt, by = ytiles[qg % 2], Byt[qg % 2]
                p.op("dve", R.reciprocal(out=rc, in_=O[:, :, 64]), reads=[bO], writes=[Brc])
                p.op("dve", R.tensor_tensor(
                    out=yt[:, :, h * 64:(h + 1) * 64], in0=O[:, :, 0:64],
                    in1=rc.unsqueeze(2).to_broadcast([128, 4, 64]), op=ALU.mult),
                    reads=[bO, Brc], writes=[by])
                if h == 7:
                    store_y(kind, qg, yt, by)

            attn_core(qk_score(QT, KT, BQ, BK), Vt, 65, BV, make_E, finish)
            return

        if dbg and dbg.get('cut') == 5:
            return
        ksum = A.alloc((4, 8), BF16)
        ksum_f = A.alloc((4, 8))
        Bks = Buf("ksum")
        for c in range(4):
            p.op("dve", R.tensor_reduce(
                out=ksum_f[:, c, :], in_=KT[:, c, :].rearrange("p (b t) -> p b t", b=8), axis=AX.X, op=ALU.add),
                reads=[BK], writes=[Bks])
        p.op("dve", R.tensor_copy(out=ksum, in_=ksum_f), reads=[Bks], writes=[Bks])
        selm = A.alloc((NT, 8, 8))
        Bsel = Buf("selm")
        g = A.alloc((8, 8)); g2 = A.alloc((8, 8)); eq = A.alloc((8, 8)); mx = A.alloc((8,))
        Bg_ = Buf("gate")
        for j in range(8, NT):
            nb = j // 2
            for h in range(8):
                gb = 6 + (h % 2)
                p.op("pe", R.matmul(
                    banks[gb][:, (h // 2) * 8:(h // 2) * 8 + 8], lhsT=QT[(h % 2) * 64:(h % 2) * 64 + 64, h // 2, j * 128:(j + 1) * 128],
                    rhs=ksum[(h % 2) * 64:(h % 2) * 64 + 64, h // 2, :], start=True, stop=True),
                    reads=[BQ, Bks], writes=[PB[gb]])
            g4 = g.rearrange("p (a two) b -> p a two b", two=2)
            for par in range(2):
                p.op("dve", R.tensor_copy(out=g4[:, :, par, 0:nb],
                                          in_=banks[6 + par][:, 0:32].rearrange("p (a b) -> p a b", a=4)[:, :, 0:nb]),
                     reads=[PB[6 + par], Bg_], writes=[Bg_])
            gv, g2v, eqv = g[:, :, 0:nb], g2[:, :, 0:nb], eq[:, :, 0:nb]
            mxb = mx.unsqueeze(2).to_broadcast([128, 8, nb])
            fns = []
            cur = gv
            for r in range(2):
                fns.append(R.tensor_reduce(out=mx, in_=cur, axis=AX.X, op=ALU.max))
                fns.append(R.tensor_tensor(out=eqv, in0=cur, in1=mxb, op=ALU.is_ge))
                fns.append(R.scalar_tensor_tensor(out=g2v, in0=eqv, scalar=NEG, in1=cur, op0=ALU.mult, op1=ALU.add))
                cur = g2v
            fns.append(R.tensor_reduce(out=mx, in_=cur, axis=AX.X, op=ALU.max))
            fns.append(R.tensor_tensor(out=selm[:, j, :, 0:nb], in0=gv, in1=mxb, op=ALU.is_ge))
            chain("dve", fns, [], [Bsel, Bg_])

        if dbg and dbg.get('cut') == 4:
            return
        acc = A.alloc((4, 65))
        Bacc = Buf("acc")
        Es = [A.alloc((512,), BF16) for _ in range(3)]
        BE = [Buf("E%d" % i) for i in range(3)]
        score = qk_score(QT, KT, BQ, BK)
        items = [(qg, h, kb, kk) for qg in range(4) for h in range(8) for kb in range(2 * qg + 2) for kk in range(2)]
        score(items[0][1], items[0][0], 0, 0)
        ocnt = -1
        O = ob = started = use = None
        for idx, (qg, h, kb, kk) in enumerate(items):
            yt, by = ytiles[qg % 2], Byt[qg % 2]
            kt = 2 * kb + kk
            if kb == 0 and kk == 0:
                p.op("dve", R.memset(acc, 0.0), writes=[Bacc])
            if kk == 0:
                ocnt += 1
                ob = 2 + (ocnt % 2)
                O = banks[ob][:, 0:4 * 65].rearrange("p (q v) -> p q v", q=4)
                started = [False] * 4
                use = [qt for qt in range(4) if kb <= (4 * qg + qt) // 2]
            if idx + 1 < len(items):
                nq, nh, nkb, nkk = items[idx + 1]
                score(nh, nq, 2 * nkb + nkk, (idx + 1) % 2)
            sb = idx % 2
            E, bE = Es[idx % 3], BE[idx % 3]
            p.op("act", R.activation(out=E, in_=banks[sb][:, 0:512], func=AF.Exp, scale=0.125),
                 reads=[PB[sb]], writes=[bE])
            dj = kt - 4 * qg
            if dj >= 0:
                p.op("dve", R.tensor_tensor(out=E, in0=E, in1=trim[:, dj, :], op=ALU.mult),
                     reads=[bE, B_const], writes=[bE])
            for qt in use:
                if kt > 4 * qg + qt:
                    continue
                lastk = min(2 * kb + 1, 4 * qg + qt)
                p.op("pe", R.matmul(
                    O[:, qt, :], lhsT=E[:, qt * 128:(qt + 1) * 128], rhs=Vt[:, kt, h, :],
                    start=(not any(started)), stop=(kt == lastk), skip_group_check=True),
                    reads=[bE, BV], writes=[PB[ob]])
                started[qt] = True
            if kk == 0:
                continue
            for qt in use:
                jt = 4 * qg + qt
                bq = jt // 2
                if kb == bq or bq <= 3:
                    p.op("dve", R.tensor_tensor(out=acc[:, qt, :], in0=O[:, qt, :], in1=acc[:, qt, :], op=ALU.add),
                         reads=[PB[ob], Bacc], writes=[Bacc])
                else:
                    p.op("dve", R.scalar_tensor_tensor(
                        out=acc[:, qt, :], in0=O[:, qt, :], scalar=selm[:, jt, h, kb:kb + 1], in1=acc[:, qt, :],
                        op0=ALU.mult, op1=ALU.add),
                        reads=[PB[ob], Bacc, Bsel], writes=[Bacc])
            if kb == 2 * qg + 1:
                p.op("dve", R.reciprocal(out=rc, in_=acc[:, :, 64]), reads=[Bacc], writes=[Brc])
                p.op("dve", R.tensor_tensor(
                    out=yt[:, :, h * 64:(h + 1) * 64], in0=acc[:, :, 0:64],
                    in1=rc.unsqueeze(2).to_broadcast([128, 4, 64]), op=ALU.mult),
                    reads=[Bacc, Brc], writes=[by])
                if h == 7:
                    store_y(2, qg, yt, by)

    def ret_mixer(L):
        p.barrier()
        A.reset(PERSIST)
        Bw = Buf("w")
        wq = A.alloc((8, 512), BF16); wqs = A.alloc((8, 512), BF16)
        wk = A.alloc((8, 512), BF16); wks = A.alloc((8, 512), BF16)
        wv = A.alloc((8, 512), BF16); wg = A.alloc((8, 512), BF16)
        p.op("pool", wload(wq, w_in_d[L, :, C_RQ:C_RQ + 512]), writes=[Bw], dma=True)
        p.op("pool", wload(wqs, w_sw_d[L, :, 0:512]), writes=[Bw], dma=True)
        p.op("pool", wload(wk, w_in_d[L, :, C_RK:C_RK + 512]), writes=[Bw], dma=True)
        p.op("pool", wload(wks, w_sw_d[L, :, 512:1024]), writes=[Bw], dma=True)
        p.op("pool", wload(wv, w_in_d[L, :, C_RV:C_RV + 512]), writes=[Bw], dma=True)
        p.op("pool", wload(wg, w_in_d[L, :, C_RG:C_RG + 512]), writes=[Bw], dma=True)
        ropec = A.alloc((S,)); ropes = A.alloc((S,))
        Brope = Buf("rope")
        p.op("sp", R.dma_start(out=ropec, in_=cd["ropec"]), writes=[Brope], dma=True)
        p.op("sp", R.dma_start(out=ropes, in_=cd["ropes"]), writes=[Brope], dma=True)
        rett = A.alloc((8, 512))
        Brt = Buf("rett")
        p.op("sp", R.dma_start(out=rett, in_=cd["rett"]), writes=[Brt], dma=True)
        gain_bc = A.alloc((512,))
        Bgn = Buf("gain")
        bcast_row(gain_bc, rnorm_d[L:L + 1, :], [Bgn])
        QT = A.alloc((4, S), BF16); KT = A.alloc((4, S), BF16)
        BQ, BK, BV = Buf("QT"), Buf("KT"), Buf("Vt")
        Vt = A.alloc((NT, 8, 65), BF16)
        tmp1 = A.alloc((512,)); tmp2 = A.alloc((512,))
        Btmp = Buf("tmp")
        proj_rope(QT, BQ, wq, wqs, Bw, ropec, ropes, Brope, tmp1, tmp2, Btmp)
        proj_rope(KT, BK, wk, wks, Bw, ropec, ropes, Brope, tmp1, tmp2, Btmp)
        proj_v(Vt, BV, wv, Bw, False)
        ytiles = [A.alloc((4, 512)) for _ in range(2)]
        Byt = [Buf("yt%d" % i) for i in range(2)]
        lg = np.log1p(-np.exp2(-5.0 - np.arange(8, dtype=np.float64)))

        def make_E(h, qg, kt, psS, bS, E, bE):
            dlt = 4 * qg - kt
            cst = float(np.exp(128.0 * dlt * lg[h]) * 0.125)
            p.op("dve", R.scalar_tensor_tensor(out=E, in0=psS, scalar=cst, in1=rett[:, h, :],
                                                         op0=ALU.mult, op1=ALU.mult),
                 reads=[bS, Brt], writes=[bE])
            if dlt <= 0:
                p.op("dve", R.tensor_tensor(out=E, in0=E, in1=trim[:, -dlt, :], op=ALU.mult),
                     reads=[bE, B_const], writes=[bE])

        mean = A.alloc((8,)); var = A.alloc((8,)); cen = A.alloc((512,)); sq = A.alloc((512,)); sg = A.alloc((512,))
        Bpp = Buf("pp")
        Bsg2 = Buf("sg2")

        def finish(qg, h, O, bO):
            yt, by = ytiles[qg % 2], Byt[qg % 2]
            p.op("act", R.activation(out=yt[:, :, h * 64:(h + 1) * 64], in_=O[:, :, 0:64], func=AF.Copy),
                 reads=[bO], writes=[by])
            if h != 7:
                return
            for qt in range(4):
                j = 4 * qg + qt
                y = yt[:, qt, :]
                y3 = y.rearrange("p (h v) -> p h v", h=8)
                proj_tm(wg, j, 4 + (qt % 2), Bw)
                gp = banks[4 + (qt % 2)][:, 0:512]

                c3 = cen.rearrange("p (h v) -> p h v", h=8)
                chain("dve", [
                    R.tensor_reduce(out=mean, in_=y3, axis=AX.X, op=ALU.add),
                    R.tensor_scalar(out=mean, in0=mean, scalar1=1.0 / 64, scalar2=None, op0=ALU.mult),
                    R.tensor_tensor(out=c3, in0=y3, in1=mean.unsqueeze(2).to_broadcast([128, 8, 64]), op=ALU.subtract),
                    R.tensor_tensor(out=sq, in0=cen, in1=cen, op=ALU.mult),
                    R.tensor_reduce(out=var, in_=sq.rearrange("p (h v) -> p h v", h=8), axis=AX.X, op=ALU.add),
                ], [by], [Bpp])
                p.op("act", R.activation(out=var, in_=var, func=AF.Sqrt, bias=eps_t, scale=1.0 / 64),
                     reads=[Bpp, B_const], writes=[Bpp])
                p.op("act", R.activation(out=sg, in_=gp, func=AF.Silu),
                     reads=[PB[4 + (qt % 2)]], writes=[Bsg2])
                chain("dve", [
                    R.reciprocal(out=var, in_=var),
                    R.tensor_tensor(out=c3, in0=c3, in1=var.unsqueeze(2).to_broadcast([128, 8, 64]), op=ALU.mult),
                    R.tensor_tensor(out=cen, in0=cen, in1=gain_bc, op=ALU.mult),
                    R.tensor_tensor(out=y, in0=cen, in1=sg, op=ALU.mult),
                ], [Bgn, Bsg2], [Bpp, by])
            store_y(1, qg, yt, by)

        attn_core(qk_score(QT, KT, BQ, BK), Vt, 64, BV, make_E, finish)

    def ssd_mixer(L):
        p.barrier()
        A.reset(PERSIST)
        Bw = Buf("w")
        wz = A.alloc((8, 512), BF16)
        wx = A.alloc((8, 768), BF16)
        wdt = A.alloc((8, 8), BF16)
        p.op("pool", wload(wz, w_in_d[L, :, C_Z:C_Z + 512]), writes=[Bw], dma=True)
        p.op("pool", wload(wx, w_in_d[L, :, C_XBC:C_XBC + 768]), writes=[Bw], dma=True)
        p.op("pool", wload(wdt, w_in_d[L, :, C_DT:C_DT + 8]), writes=[Bw], dma=True)
        cw = A.alloc((6, 4)); cb = A.alloc((6,))
        Bcw = Buf("cw")
        for k in range(4):
            p.op("sp", R.dma_start(out=cw[:, :, k], in_=convw_d[L, k].rearrange("(c p) -> p c", p=128),
                                                  allow_slow_non_contiguous=True), writes=[Bcw], dma=True)
        p.op("sp", R.dma_start(out=cb, in_=convb_d[L].rearrange("(c p) -> p c", p=128),
                                         allow_slow_non_contiguous=True), writes=[Bcw], dma=True)
        small = A.alloc((4, 8))
        Bsm = Buf("small")
        bcast_row(small[:, 0, :], dtb_d[L:L + 1, :], [Bsm])
        bcast_row(small[:, 1, :], alog_d[L:L + 1, :], [Bsm])
        bcast_row(small[:, 2, :], sd_d[L:L + 1, :], [Bsm])
        p.op("act", R.activation(out=small[:, 1, :], in_=small[:, 1, :], func=AF.Exp), reads=[Bsm], writes=[Bsm])
        gain_bc = A.alloc((512,))
        Bgn = Buf("gain")
        bcast_row(gain_bc, snorm_d[L:L + 1, :], [Bgn])
        cumm = A.alloc((3968,))
        hsel = A.alloc((8, 128))
        Bcm = Buf("cumm")
        p.op("sp", R.dma_start(out=cumm, in_=cd["cumm"]), writes=[Bcm], dma=True)
        p.op("sp", R.dma_start(out=hsel[0:8], in_=cd["hsel"]), writes=[Bcm], dma=True)

        if dbg and dbg.get('cut') == 1:
            return
        stage = A.alloc((4 + S,))
        cacc = A.alloc((S,))
        Bst, Bca = Buf("stage"), Buf("cacc")
        xs_f = A.alloc((NT, 512))
        Vt = A.alloc((NT, 8, 65), BF16)
        BT = A.alloc((S,), BF16); CT = A.alloc((S,), BF16)
        BV, BBC, Bxf = Buf("Vt"), Buf("BC"), Buf("xsf")
        p.op("dve", R.memset(stage[:, 0:4], 0.0), writes=[Bst])
        for c in range(6):
            for tg in range(4):
                pb = 4 + (tg % 2)
                proj_fm(wx[:, :, c * 128:(c + 1) * 128], tg, pb, Bw)
                p.op("act", R.activation(out=stage[:, 4 + tg * 512:4 + (tg + 1) * 512],
                                                                 in_=banks[pb][:, 0:512], func=AF.Copy),
                     reads=[PB[pb]], writes=[Bst])

            fns = [R.tensor_scalar(out=cacc, in0=stage[:, 1:1 + S], scalar1=cw[:, c, 0:1], scalar2=cb[:, c:c + 1],
                                                  op0=ALU.mult, op1=ALU.add)]
            for k in range(1, 4):
                fns.append(R.scalar_tensor_tensor(out=cacc, in0=stage[:, 1 + k:1 + k + S], scalar=cw[:, c, k:k + 1],
                                                                      in1=cacc, op0=ALU.mult, op1=ALU.add))
            chain("dve", fns, [Bst, Bcw], [Bca])
            if c < 4:
                p.op("act", R.activation(out=cacc, in_=cacc, func=AF.Silu), reads=[Bca], writes=[Bca])
                for j in range(NT):
                    pb = 6 + (j % 2)
                    p.op("pe", R.transpose(out=banks[pb][:, 0:128], in_=cacc[:, j * 128:(j + 1) * 128],
                                                                 identity=ident_f),
                         reads=[Bca, B_const], writes=[PB[pb]])
                    p.op("act", R.activation(
                        out=Vt[:, j, 2 * c:2 * c + 2, 0:64], in_=banks[pb][:, 0:128].rearrange("p (h v) -> p h v", h=2), func=AF.Copy),
                        reads=[PB[pb]], writes=[BV])
                    p.op("dve", R.tensor_copy(out=xs_f[:, j, c * 128:(c + 1) * 128], in_=banks[pb][:, 0:128]),
                         reads=[PB[pb]], writes=[Bxf])
            else:
                dst = BT if c == 4 else CT
                p.op("act", R.activation(out=dst, in_=cacc, func=AF.Silu), reads=[Bca], writes=[BBC])

        if dbg and dbg.get('cut') == 2:
            return
        dt_tm = A.alloc((NT, 8)); la = A.alloc((NT, 8)); negA = A.alloc((NT, 8))
        A_fm = A.alloc((S,))
        Bdt, BAf = Buf("dt"), Buf("Afm")
        for j in range(NT):
            for kc in range(8):
                p.op("pe", R.matmul(banks[6][:, j * 8:(j + 1) * 8], lhsT=hnT[:, kc, j * 128:(j + 1) * 128],
                                                        rhs=wdt[:, kc, :], start=(kc == 0), stop=(kc == 7), skip_group_check=True),
                     reads=[Bw, B_hnT], writes=[PB[6]])

        def dtf(e):
            e.tensor_tensor(out=dt_tm, in0=banks[6][:, 0:128].rearrange("p (j h) -> p j h", j=NT),
                            in1=small[:, 0:1, :].to_broadcast([128, NT, 8]), op=ALU.add)
            return e
        p.op("dve", R.tensor_tensor(out=dt_tm, in0=banks[6][:, 0:128].rearrange("p (j h) -> p j h", j=NT),
                                              in1=small[:, 0:1, :].to_broadcast([128, NT, 8]), op=ALU.add),
             reads=[PB[6], Bsm], writes=[Bdt])
        p.op("act", R.activation(out=dt_tm, in_=dt_tm, func=AF.Exp), reads=[Bdt], writes=[Bdt])
        p.op("act", R.activation(out=dt_tm, in_=dt_tm, func=AF.Ln, bias=ones_t, scale=1.0), reads=[Bdt, B_const], writes=[Bdt])
        p.op("dve", R.scalar_tensor_tensor(out=la, in0=dt_tm, scalar=-1.0, in1=small[:, 1:2, :].to_broadcast([128, NT, 8]),
                                                     op0=ALU.mult, op1=ALU.mult),
             reads=[Bdt, Bsm], writes=[Bdt])
        for gq in range(4):
            n_i = 4 * gq + 4
            for i in range(n_i):
                o0 = 1920 - 128 * i + 512 * gq
                p.op("pe", R.matmul(banks[7][0:8, 0:512], lhsT=la[:, i, :], rhs=cumm[:, o0:o0 + 512],
                                                                  start=(i == 0), stop=(i == n_i - 1)),
                     reads=[Bdt, Bcm], writes=[PB[7]])
            p.op("act", R.activation(out=A_fm[0:8, gq * 512:(gq + 1) * 512], in_=banks[7][0:8, 0:512], func=AF.Copy),
                 reads=[PB[7]], writes=[BAf])
        for j in range(NT):
            for i in range(j + 1):
                o0 = 1920 - 128 * i + 128 * j
                p.op("pe", R.matmul(banks[6][:, j * 8:(j + 1) * 8], lhsT=cumm[:, o0:o0 + 128], rhs=la[:, i, :],
                                                             start=(i == 0), stop=(i == j), skip_group_check=True),
                     reads=[Bdt, Bcm], writes=[PB[6]])
        p.op("dve", R.tensor_scalar(out=negA, in0=banks[6][:, 0:128].rearrange("p (j h) -> p j h", j=NT),
                                              scalar1=-1.0, scalar2=None, op0=ALU.mult),
             reads=[PB[6]], writes=[Bdt])

        Btap = A.alloc((512,)); Ctap = A.alloc((512,))
        BBt = Buf("Btap")
        p.op("dve", R.tensor_copy(out=Btap, in_=BT[:, 0:512]), reads=[BBC], writes=[BBt])
        p.op("dve", R.tensor_copy(out=Ctap, in_=CT[:, 0:512]), reads=[BBC], writes=[BBt])
        tap(Btap, [BBt], 3200)
        tap(Ctap, [BBt], 3200 + 256)
        tap(xs_f[:, 0, :], [Bxf], 0)
        tap(dt_tm.rearrange("p j h -> p (j h)"), [Bdt], 512)
        tap(negA.rearrange("p j h -> p (j h)"), [Bdt], 640)
        tap(A_fm[0:8, 0:512], [BAf], 768)
        tap(la.rearrange("p j h -> p (j h)"), [Bdt], 1280)
        if dbg and dbg.get('cut') == 3:
            return
        ytiles = [A.alloc((4, 512)) for _ in range(2)]
        Byt = [Buf("yt%d" % i) for i in range(2)]
        Abc = A.alloc((512,))
        BAbc = Buf("Abc")
        dcl = [A.alloc((512,)) for _ in range(2)]
        Bdcl = [Buf("dcl%d" % i) for i in range(2)]
        ecnt = [0]
        KQ = [None]

        def make_E(h, qg, kt, psS, bS, E, bE):
            if kt == 0:
                p.op("pe", R.matmul(banks[7][:, 0:512], lhsT=hsel[0:8, h, :], rhs=A_fm[0:8, qg * 512:(qg + 1) * 512],
                                              start=True, stop=True), reads=[BAf, Bcm], writes=[PB[7]])
                p.op("act", R.activation(out=Abc, in_=banks[7][:, 0:512], func=AF.Copy), reads=[PB[7]], writes=[BAbc])
            d_, bd = dcl[ecnt[0] % 2], Bdcl[ecnt[0] % 2]
            ecnt[0] += 1
            if kt < 4 * qg:
                p.op("act", R.activation(out=d_, in_=Abc, func=AF.Exp, bias=negA[:, kt, h:h + 1], scale=1.0),
                     reads=[BAbc, Bdt], writes=[bd])
            else:
                p.op("dve", R.tensor_scalar(out=d_, in0=Abc, scalar1=negA[:, kt, h:h + 1], scalar2=0.0,
                                            op0=ALU.add, op1=ALU.min), reads=[BAbc, Bdt], writes=[bd])
                p.op("act", R.activation(out=d_, in_=d_, func=AF.Exp), reads=[bd], writes=[bd])
            p.op("dve", R.scalar_tensor_tensor(out=E, in0=psS, scalar=dt_tm[:, kt, h:h + 1], in1=d_,
                                                         op0=ALU.mult, op1=ALU.mult), reads=[bS, bd, Bdt], writes=[bE])
            dj = kt - 4 * qg
            if dj >= 0:
                p.op("dve", R.tensor_tensor(out=E, in0=E, in1=trim[:, dj, :], op=ALU.mult),
                     reads=[bE, B_const], writes=[bE])
            if h == 0 and qg == 0 and kt == 0:
                tap(d_, [bd], 1536)
                p.op("dve", R.tensor_copy(out=Etap, in_=E), reads=[bE], writes=[BEt])
                tap(Etap, [BEt], 2048)
                p.op("dve", R.tensor_copy(out=Etap2, in_=psS), reads=[bS], writes=[BEt2])
                tap(Etap2, [BEt2], 2560)

        Etap = A.alloc((512,)); Etap2 = A.alloc((512,))
        BEt, BEt2 = Buf("Et"), Buf("Et2")
        sz = A.alloc((512,)); ssq = A.alloc((4,)); junk = A.alloc((512,), BF16)
        Bpp = Buf("pp")

        def finish(qg, h, O, bO):
            yt, by = ytiles[qg % 2], Byt[qg % 2]
            for qt in range(4):
                j = 4 * qg + qt
                p.op("dve", R.scalar_tensor_tensor(
                    out=yt[:, qt, h * 64:(h + 1) * 64], in0=xs_f[:, j, h * 64:(h + 1) * 64], scalar=small[:, 2, h:h + 1],
                    in1=O[:, qt, 0:64], op0=ALU.mult, op1=ALU.add), reads=[bO, Bxf, Bsm], writes=[by])
            if h == 0 and qg == 0:
                tap(yt[:, 0, 0:64], [by], 3072)
                tap(yt[:, 1, 0:64], [by], 3136)
            if h != 7:
                return
            for qt in range(4):
                j = 4 * qg + qt
                y = yt[:, qt, :]
                pb = 4 + (qt % 2)
                proj_tm(wz, j, pb, Bw)
                p.op("act", R.activation(out=sz, in_=banks[pb][:, 0:512], func=AF.Silu), reads=[PB[pb]], writes=[Bpp])
                p.op("dve", R.tensor_tensor(out=y, in0=y, in1=sz, op=ALU.mult), reads=[Bpp, by], writes=[by])
                p.op("act", R.activation(out=junk, in_=y, func=AF.Square, accum_out=ssq[:, qt:qt + 1]),
                     reads=[by], writes=[Bpp])
                p.op("act", R.activation(out=ssq[:, qt:qt + 1], in_=ssq[:, qt:qt + 1], func=AF.Sqrt, bias=eps_t, scale=1.0 / 512),
                     reads=[Bpp, B_const], writes=[Bpp])

                chain("dve", [
                    R.reciprocal(out=ssq[:, qt:qt + 1], in_=ssq[:, qt:qt + 1]),
                    R.scalar_tensor_tensor(out=y, in0=y, scalar=ssq[:, qt:qt + 1], in1=gain_bc, op0=ALU.mult, op1=ALU.mult),
                ], [Bgn], [Bpp, by])
            store_y(0, qg, yt, by)

        def bc_score(h, qg, kt, sb):
            gp = (h // 4) * 64
            p.op("pe", R.matmul(banks[sb][:, 0:512], lhsT=BT[gp:gp + 64, kt * 128:(kt + 1) * 128],
                                rhs=CT[gp:gp + 64, qg * 512:(qg + 1) * 512], start=True, stop=True),
                 reads=[BBC], writes=[PB[sb]])
        attn_core(bc_score, Vt, 64, BV, make_E, finish)

    def merge_phase(L):
        p.barrier()
        A.reset(PERSIST)
        Bw = Buf("w")
        wb = [A.alloc((4, D), BF16) for _ in range(4)]
        wg = A.alloc((8, 4 * D), BF16)
        wo = A.alloc((8, D), BF16)
        for i in range(4):
            p.op("pool", R.dma_start(out=wb[i], in_=wbr_d[L, i].rearrange("(c p) n -> p c n", p=128)),
                 writes=[Bw], dma=True)
        for i in range(4):
            p.op("pool", wload(wg[:, :, i * D:(i + 1) * D], w_in_d[L, :, C_G + i * D:C_G + (i + 1) * D]), writes=[Bw], dma=True)
        p.op("pool", wload(wo, wout_d[L]), writes=[Bw], dma=True)
        yin = [A.alloc((4, 512)) for _ in range(2)]
        Byin = [Buf("yin%d" % i) for i in range(2)]
        yb = A.alloc((4, 512), BF16)
        Byb = Buf("yb")
        yT = A.alloc((16, 128), BF16)
        ByT = Buf("yT")
        mg = A.alloc((D,)); sgm = A.alloc((512,)); mgb = A.alloc((D,), BF16); mT = A.alloc((8, 128), BF16)
        Bmg, Bsg, Bmgb, BmT = Buf("mg"), Buf("sg"), Buf("mgb"), Buf("mT")
        hts = [A.alloc((D,)) for _ in range(2)]
        Bht = [Buf("ht%d" % i) for i in range(2)]
        for j in range(NT):
            rows = slice(j * 128, (j + 1) * 128)
            yi, byi = yin[j % 2], Byin[j % 2]
            ht, bh = hts[j % 2], Bht[j % 2]
            p.op("sp", R.dma_start(out=yi, in_=ymix[:, rows, :].rearrange("i p c -> p i c")),
                 reads=B_ymix, writes=[byi], dma=True)
            p.op("sp", R.dma_start(out=ht, in_=hbuf[rows, :]), reads=[B_h], writes=[bh], dma=True)
            p.op("act", R.activation(out=yb, in_=yi, func=AF.Copy), reads=[byi], writes=[Byb])
            for hf in range(2):
                pb = 6 + hf
                pst = bank_bf(pb)
                for q in range(8):
                    ic = hf * 8 + q
                    p.op("pe", R.transpose(
                        out=pst[:, q * 128:(q + 1) * 128], in_=yb[:, ic // 4, (ic % 4) * 128:(ic % 4 + 1) * 128], identity=ident_b),
                        reads=[Byb, B_const], writes=[PB[pb]])
                p.op("act", R.activation(out=yT[:, hf * 8:(hf + 1) * 8, :],
                                                                   in_=pst.rearrange("p (c t) -> p c t", c=8), func=AF.Copy),
                     reads=[PB[pb]], writes=[ByT])
            for i in range(4):
                for hf in range(2):
                    cs = slice(hf * 512, (hf + 1) * 512)
                    for c in range(4):
                        p.op("pe", R.matmul(banks[0 + hf][:, 0:512], lhsT=yT[:, i * 4 + c, :], rhs=wb[i][:, c, cs],
                                                                       start=(c == 0), stop=(c == 3)),
                             reads=[ByT, Bw], writes=[PB[0 + hf]])
                    for kc in range(8):
                        p.op("pe", R.matmul(
                            banks[2 + hf][:, 0:512], lhsT=hnT[:, kc, j * 128:(j + 1) * 128],
                            rhs=wg[:, kc, i * D + hf * 512:i * D + (hf + 1) * 512], start=(kc == 0), stop=(kc == 7)),
                            reads=[B_hnT, Bw], writes=[PB[2 + hf]])
                    p.op("act", R.activation(out=sgm, in_=banks[2 + hf][:, 0:512], func=AF.Sigmoid),
                         reads=[PB[2 + hf]], writes=[Bsg])
                    if i == 0:
                        p.op("dve", R.tensor_tensor(out=mg[:, cs], in0=banks[0 + hf][:, 0:512], in1=sgm, op=ALU.mult),
                             reads=[PB[0 + hf], Bsg], writes=[Bmg])
                    else:
                        chain("dve", [
                            R.tensor_tensor(out=sgm, in0=banks[0 + hf][:, 0:512], in1=sgm, op=ALU.mult),
                            R.tensor_tensor(out=mg[:, cs], in0=mg[:, cs], in1=sgm, op=ALU.add),
                        ], [PB[0 + hf]], [Bsg, Bmg])
            resid_out(mg, Bmg, mgb, Bmgb, mT, BmT, wo, Bw, ht, bh, rows)

    def resid_out(src_f, Bsrc, srcb, Bsrcb, sT, BsT, wo, Bw, ht, bh, rows):
        p.op("act", R.activation(out=srcb, in_=src_f, func=AF.Copy), reads=[Bsrc], writes=[Bsrcb])
        pst = bank_bf(6)
        for c in range(8):
            p.op("pe", R.transpose(out=pst[:, c * 128:(c + 1) * 128], in_=srcb[:, c * 128:(c + 1) * 128], identity=ident_b),
                 reads=[Bsrcb, B_const], writes=[PB[6]])
        p.op("act", R.activation(out=sT, in_=pst.rearrange("p (c t) -> p c t", c=8), func=AF.Copy),
             reads=[PB[6]], writes=[BsT])
        for hf in range(2):
            cs = slice(hf * 512, (hf + 1) * 512)
            for kc in range(8):
                p.op("pe", R.matmul(banks[4 + hf][:, 0:512], lhsT=sT[:, kc, :], rhs=wo[:, kc, cs],
                                                                   start=(kc == 0), stop=(kc == 7)),
                     reads=[BsT, Bw], writes=[PB[4 + hf]])
            p.op("dve", R.tensor_tensor(out=ht[:, cs], in0=ht[:, cs], in1=banks[4 + hf][:, 0:512], op=ALU.add),
                 reads=[PB[4 + hf], bh], writes=[bh])
        p.op("sp", R.dma_start(out=hbuf[rows, :], in_=ht), reads=[bh], writes=[B_h], dma=True)

    def cross_phase(L):
        p.barrier()
        A.reset(PERSIST)
        Bw = Buf("w")
        wq = A.alloc((8, D), BF16); wkv = A.alloc((8, 2 * D), BF16); wo = A.alloc((8, D), BF16)
        p.op("pool", wload(wq, wxq_d[L]), writes=[Bw], dma=True)
        p.op("pool", wload(wkv, wxkv_d[L]), writes=[Bw], dma=True)
        p.op("pool", wload(wo, wxo_d[L]), writes=[Bw], dma=True)
        memb = A.alloc((2, D), BF16)
        Bmem = Buf("mem")
        p.op("pool", R.dma_start(out=memb, in_=mem_d.rearrange("(m p) d -> p m d", p=128)), writes=[Bmem], dma=True)
        memT = A.alloc((8, 256), BF16)
        BmemT = Buf("memT")
        for m in range(2):
            pst = bank_bf(6 + m)
            for c in range(8):
                p.op("pe", R.transpose(out=pst[:, c * 128:(c + 1) * 128], in_=memb[:, m, c * 128:(c + 1) * 128],
                                                                   identity=ident_b), reads=[Bmem, B_const], writes=[PB[6 + m]])
            p.op("act", R.activation(out=memT[:, :, m * 128:(m + 1) * 128],
                                                             in_=pst.rearrange("p (c t) -> p c t", c=8), func=AF.Copy),
                 reads=[PB[6 + m]], writes=[BmemT])
        KxT = A.alloc((8, 256), BF16)
        Vx = A.alloc((2, D), BF16)
        BKx, BVx = Buf("KxT"), Buf("Vx")
        for c in range(8):
            pb = 4 + (c % 2)
            for kc in range(8):
                p.op("pe", R.matmul(banks[pb][:, 0:256], lhsT=wkv[:, kc, c * 128:(c + 1) * 128], rhs=memT[:, kc, :],
                                                              start=(kc == 0), stop=(kc == 7)), reads=[Bw, BmemT], writes=[PB[pb]])
            p.op("act", R.activation(out=KxT[:, c, :], in_=banks[pb][:, 0:256], func=AF.Copy),
                 reads=[PB[pb]], writes=[BKx])
        for m in range(2):
            for hf in range(2):
                pb = 4 + hf
                for kc in range(8):
                    p.op("pe", R.matmul(
                        banks[pb][:, 0:512], lhsT=memT[:, kc, m * 128:(m + 1) * 128], rhs=wkv[:, kc, D + hf * 512:D + (hf + 1) * 512],
                        start=(kc == 0), stop=(kc == 7)), reads=[Bw, BmemT], writes=[PB[pb]])
                p.op("act", R.activation(out=Vx[:, m, hf * 512:(hf + 1) * 512], in_=banks[pb][:, 0:512], func=AF.Copy),
                     reads=[PB[pb]], writes=[BVx])
        QxT = A.alloc((8, S), BF16)
        BQx = Buf("QxT")
        for c in range(8):
            for tg in range(4):
                pb = 4 + (tg % 2)
                proj_fm(wq[:, :, c * 128:(c + 1) * 128], tg, pb, Bw)
                p.op("act", R.activation(out=QxT[:, c, tg * 512:(tg + 1) * 512], in_=banks[pb][:, 0:512], func=AF.Copy),
                     reads=[PB[pb]], writes=[BQx])
        Es = [A.alloc((512,), BF16) for _ in range(3)]
        BE = [Buf("E%d" % i) for i in range(3)]
        xo = A.alloc((4, D))
        Bxo = Buf("xo")
        rc = A.alloc((4,))
        Brc = Buf("rc")
        xob = A.alloc((D,), BF16); xT = A.alloc((8, 128), BF16)
        Bxob, BxT = Buf("xob"), Buf("xT")
        hts = [A.alloc((D,)) for _ in range(2)]
        Bht = [Buf("ht%d" % i) for i in range(2)]
        cnt = 0
        for qg in range(4):
            for hx in range(4):
                Oa = banks[2][:, 0:512].rearrange("p (q v) -> p q v", q=2)
                Ob = banks[3][:, 0:512].rearrange("p (q v) -> p q v", q=2)
                sm = banks[7][:, 0:4]
                for mt in range(2):
                    sb = cnt % 2
                    E, bE = Es[cnt % 3], BE[cnt % 3]
                    cnt += 1
                    for cc in range(2):
                        p.op("pe", R.matmul(
                            banks[sb][:, 0:512], lhsT=KxT[:, 2 * hx + cc, mt * 128:(mt + 1) * 128],
                            rhs=QxT[:, 2 * hx + cc, qg * 512:(qg + 1) * 512], start=(cc == 0), stop=(cc == 1)),
                            reads=[BKx, BQx], writes=[PB[sb]])
                    p.op("act", R.activation(out=E, in_=banks[sb][:, 0:512], func=AF.Exp, scale=1.0 / 16),
                         reads=[PB[sb]], writes=[bE])
                    for qt in range(4):
                        Ot, ob = (Oa, 2) if qt < 2 else (Ob, 3)
                        p.op("pe", R.matmul(
                            Ot[:, qt % 2, :], lhsT=E[:, qt * 128:(qt + 1) * 128], rhs=Vx[:, mt, hx * 256:(hx + 1) * 256],
                            start=(mt == 0 and qt % 2 == 0), stop=(mt == 1), skip_group_check=True), reads=[bE, BVx], writes=[PB[ob]])
                        p.op("pe", R.matmul(
                            sm[:, qt:qt + 1], lhsT=E[:, qt * 128:(qt + 1) * 128], rhs=ones_bf[:, 0:1],
                            start=(mt == 0 and qt == 0), stop=(mt == 1), skip_group_check=True), reads=[bE, B_const], writes=[PB[7]])
                p.op("dve", R.reciprocal(out=rc, in_=banks[7][:, 0:4]), reads=[PB[7]], writes=[Brc])
                for qt in range(4):
                    Ot, ob = (Oa, 2) if qt < 2 else (Ob, 3)
                    p.op("act", R.activation(
                        out=xo[:, qt, hx * 256:(hx + 1) * 256], in_=Ot[:, qt % 2, :], func=AF.Copy, scale=rc[:, qt:qt + 1]),
                        reads=[PB[ob], Brc], writes=[Bxo])
            for qt in range(4):
                j = 4 * qg + qt
                rows = slice(j * 128, (j + 1) * 128)
                ht, bh = hts[j % 2], Bht[j % 2]
                p.op("sp", R.dma_start(out=ht, in_=hbuf[rows, :]), reads=[B_h], writes=[bh], dma=True)
                resid_out(xo[:, qt, :], Bxo, xob, Bxob, xT, BxT, wo, Bw, ht, bh, rows)

    ones_bf = A.alloc((8,), BF16)
    p.op("dve", R.memset(ones_bf, 1.0), writes=[B_const])
    PERSIST = A.mark()

    def peer_phase(L):
        p.barrier()
        A.reset(PERSIST)
        m0 = A.mark()
        AT = A.alloc((S,)); BTt = A.alloc((S,)); WT = A.alloc((S,))
        BABW = Buf("ABW")
        m1 = A.mark()
        Bw = Buf("w")
        wpq = A.alloc((8, 2048), BF16)
        p.op("pool", wload(wpq, wpq_d[L]), writes=[Bw], dma=True)
        skT = A.alloc((16, 128), BF16)
        p.op("pool", R.dma_start(out=skT, in_=skT_d[L].rearrange("c q k -> q c k")), writes=[Bw], dma=True)
        QpT = A.alloc((16, 128), BF16)
        BQp = Buf("QpT")
        ssbs = [A.alloc((16, 128)) for _ in range(2)]
        Bss_ = [Buf("ssb%d" % i) for i in range(2)]
        scr1 = A.alloc((16, 128))
        V12 = A.alloc((16, 16)); I12u = A.alloc((16, 16), U32); IFt = A.alloc((16, 16))
        Bc = [Buf("c%d" % c) for c in range(16)]
        cand = A.alloc((8, 256)); scr2 = A.alloc((8, 256))
        best = A.alloc((8, 16)); posu = A.alloc((8, 16), U32)
        Bh = [Buf("h%d" % h) for h in range(8)]
        Bcand, Bb = Buf("cand"), Buf("batch")
        posf = A.alloc((128,)); j1f = A.alloc((128,)); j2f = A.alloc((128,))
        t3 = A.alloc((128, 16)); t3b = A.alloc((128, 16))
        negm = A.alloc((8,)); zs = A.alloc((8,)); wex = A.alloc((128,)); bm = A.alloc((128,))
        Aidx = A.alloc((128,)); Bidx = A.alloc((128,)); Wgt = A.alloc((128,))
        Babw = Buf("abw_tm")
        V12v = V12.rearrange("p (h two) k -> p h two k", two=2)
        IFv = IFt.rearrange("p (h two) k -> p h two k", two=2)
        best_f = best.rearrange("p h k -> p (h k)")
        for j in range(NT):
            ssb, Bs = ssbs[j % 2], Bss_[j % 2]
            for c in range(16):
                pb = c // 4
                for kc in range(8):
                    p.op("pe", R.matmul(
                        banks[pb][:, (c % 4) * 128:(c % 4 + 1) * 128], lhsT=wpq[:, kc, c * 128:(c + 1) * 128],
                        rhs=hnT[:, kc, j * 128:(j + 1) * 128], start=(kc == 0), stop=(kc == 7), skip_group_check=True),
                        reads=[Bw, B_hnT], writes=[PB[pb]])
            for pb in range(4):
                p.op("act", R.activation(out=QpT[:, pb * 4:(pb + 1) * 4, :],
                                         in_=banks[pb][:, 0:512].rearrange("p (c t) -> p c t", c=4), func=AF.Copy),
                     reads=[PB[pb]], writes=[BQp])
            for c in range(16):
                pb = 4 + c // 4
                p.op("pe", R.matmul(banks[pb][:, (c % 4) * 128:(c % 4 + 1) * 128], lhsT=QpT[:, c, :], rhs=skT[:, c, :],
                                    start=True, stop=True, skip_group_check=True),
                     reads=[BQp, Bw], writes=[PB[pb]])
            for pb in range(4):
                p.op("act", R.activation(out=ssb[:, pb * 4:(pb + 1) * 4, :],
                                         in_=banks[4 + pb][:, 0:512].rearrange("p (c t) -> p c t", c=4), func=AF.Copy),
                     reads=[PB[4 + pb]], writes=[Bs])
            for step in range(5):
                for c in range(16):
                    sc = ssb[:, c, :]
                    if step == 0:
                        f = R.max(out=V12[:, c, 0:8], in_=sc)
                    elif step == 1:
                        f = R.max_index(out=I12u[:, c, 0:8], in_max=V12[:, c, 0:8], in_values=sc)
                    elif step == 2:
                        f = R.match_replace(out=scr1[:, c, :], in_to_replace=V12[:, c, 0:8], in_values=sc, imm_value=NEG)
                    elif step == 3:
                        f = R.max(out=V12[:, c, 8:16], in_=scr1[:, c, :])
                    else:
                        f = R.max_index(out=I12u[:, c, 8:16], in_max=V12[:, c, 8:16], in_values=scr1[:, c, :])
                    p.op("dve", f, reads=[Bs, Bc[c]], writes=[Bc[c]])
            p.op("dve", R.tensor_copy(out=IFt, in_=I12u), reads=Bc, writes=[Bb])
            p.op("dve", R.tensor_tensor(out=cand.rearrange("p h (a b) -> p h a b", a=16),
                                        in0=V12v[:, :, 0, :].unsqueeze(3).to_broadcast([128, 8, 16, 16]),
                                        in1=V12v[:, :, 1, :].unsqueeze(2).to_broadcast([128, 8, 16, 16]), op=ALU.add),
                 reads=Bc, writes=[Bcand])
            for step in range(5):
                for h in range(8):
                    cdh = cand[:, h, :]
                    if step == 0:
                        f = R.max(out=best[:, h, 0:8], in_=cdh)
                    elif step == 1:
                        f = R.max_index(out=posu[:, h, 0:8], in_max=best[:, h, 0:8], in_values=cdh)
                    elif step == 2:
                        f = R.match_replace(out=scr2[:, h, :], in_to_replace=best[:, h, 0:8], in_values=cdh, imm_value=NEG)
                    elif step == 3:
                        f = R.max(out=best[:, h, 8:16], in_=scr2[:, h, :])
                    else:
                        f = R.max_index(out=posu[:, h, 8:16], in_max=best[:, h, 8:16], in_values=scr2[:, h, :])
                    p.op("dve", f, reads=[Bcand, Bh[h]], writes=[Bh[h]])
            bops = [
                R.tensor_copy(out=posf, in_=posu.rearrange("p h k -> p (h k)")),
                R.tensor_tensor(out=t3[:, :, 0:15], in0=posf.unsqueeze(2).to_broadcast([128, 128, 15]),
                                in1=thr15.unsqueeze(1).to_broadcast([128, 128, 15]), op=ALU.is_ge),
                R.tensor_reduce(out=j1f, in_=t3[:, :, 0:15], axis=AX.X, op=ALU.add),
                R.scalar_tensor_tensor(out=j2f, in0=j1f, scalar=-16.0, in1=posf, op0=ALU.mult, op1=ALU.add),
            ]
            for jf, half, dst in ((j1f, 0, Aidx), (j2f, 1, Bidx)):
                bops += [
                    R.tensor_tensor(out=t3, in0=jf.unsqueeze(2).to_broadcast([128, 128, 16]),
                                    in1=iota16.unsqueeze(1).to_broadcast([128, 128, 16]), op=ALU.is_equal),
                    R.tensor_tensor(out=t3b.rearrange("p (h k) j -> p h k j", h=8), in0=t3.rearrange("p (h k) j -> p h k j", h=8),
                                    in1=IFv[:, :, half, :].unsqueeze(2).to_broadcast([128, 8, 16, 16]), op=ALU.mult),
                    R.tensor_reduce(out=dst, in_=t3b, axis=AX.X, op=ALU.add),
                ]
            bops += [
                R.tensor_scalar(out=negm, in0=best[:, :, 0], scalar1=-1.0, scalar2=None, op0=ALU.mult),
                R.tensor_tensor(out=bm.rearrange("p (h k) -> p h k", h=8), in0=best,
                                in1=negm.unsqueeze(2).to_broadcast([128, 8, 16]), op=ALU.add),
            ]
            for f in bops:
                p.op("dve", f, reads=Bh + [Bb, B_const], writes=[Bb, Babw])
            p.op("act", R.activation(out=wex, in_=bm, func=AF.Exp), reads=[Bb], writes=[Bb])
            for f in [
                R.tensor_reduce(out=zs, in_=wex.rearrange("p (h k) -> p h k", h=8), axis=AX.X, op=ALU.add),
                R.reciprocal(out=zs, in_=zs),
                R.tensor_tensor(out=Wgt.rearrange("p (h k) -> p h k", h=8), in0=wex.rearrange("p (h k) -> p h k", h=8),
                                in1=zs.unsqueeze(2).to_broadcast([128, 8, 16]), op=ALU.mult),
            ]:
                p.op("dve", f, reads=[Bb], writes=[Bb, Babw])
            for q, (srcv, dstv) in enumerate(((Aidx, AT), (Bidx, BTt), (Wgt, WT))):
                p.op("pe", R.transpose(out=banks[q][:, 0:128], in_=srcv, identity=ident_f),
                     reads=[Babw, B_const], writes=[PB[q]])
                p.op("act", R.activation(out=dstv[:, j * 128:(j + 1) * 128], in_=banks[q][:, 0:128], func=AF.Copy),
                     reads=[PB[q]], writes=[BABW])

        if dbg and dbg.get('cut') == 11:
            return
        p.barrier()
        A.reset(m1)
        TG = 256
        wbig = [A.alloc((128, TG), BF16) for _ in range(2)]
        Bwb = [Buf("wbig%d" % i) for i in range(2)]
        Pm = [A.alloc((128,), BF16) for _ in range(4)]
        Qm = [A.alloc((128,), BF16) for _ in range(4)]
        BPQ = [Buf("PQ%d" % i) for i in range(4)]
        B_wd = Buf("wd")
        for tg in range(S // TG):
            wbg, bwb = wbig[tg % 2], Bwb[tg % 2]
            for t4 in range(TG // 4):
                pb = t4 % 4
                for u in range(4):
                    t = tg * TG + t4 * 4 + u
                    s4 = (t4 * 4 + u) % 4
                    Pt, Qt, bpq = Pm[s4], Qm[s4], BPQ[s4]

                    def mk(e, t=t, Pt=Pt, Qt=Qt):
                        e.tensor_scalar(out=Pt, in0=iota_row, scalar1=AT[:, t:t + 1], scalar2=None, op0=ALU.is_equal)
                        return e.tensor_scalar(out=Qt, in0=iota_row, scalar1=BTt[:, t:t + 1], scalar2=WT[:, t:t + 1],
                                               op0=ALU.is_equal, op1=ALU.mult)
                    p.op("dve", mk, reads=[BABW, B_const], writes=[bpq])
                    p.op("pe", R.matmul(banks[pb][:, u * 128:(u + 1) * 128], lhsT=Qt, rhs=Pt,
                                                                           start=True, stop=True, skip_group_check=True),
                         reads=[bpq], writes=[PB[pb]])
                p.op("act", R.activation(
                    out=wbg[:, :, t4 * 4:(t4 + 1) * 4], in_=banks[pb][:, 0:512].rearrange("p (u i) -> p i u", u=4), func=AF.Copy),
                    reads=[PB[pb]], writes=[bwb])
            for q4 in range(4):
                p.op("sp", R.dma_start(
                    out=wd[q4 * 32:(q4 + 1) * 32, :, tg * TG:(tg + 1) * TG].rearrange("a p t -> p a t"),
                    in_=wbg[:, q4 * 32:(q4 + 1) * 32, :]),
                    reads=[bwb], writes=[B_wd], dma=True)

        if dbg and dbg.get('cut') == 12:
            return
        p.barrier()
        A.reset(m0)
        yacc = A.alloc((8, S))
        Byacc = Buf("yacc")
        NB = 4
        wblk = [A.alloc((NB, S), BF16) for _ in range(2)]
        ublk = [A.alloc((8, NB * 128), BF16) for _ in range(2)]
        vblk = [A.alloc((NB, D), BF16) for _ in range(2)]
        Bblk = [Buf("blk%d" % i) for i in range(2)]
        Gs = [A.alloc((512,), BF16) for _ in range(3)]
        BG = [Buf("G%d" % i) for i in range(3)]
        GWs = [A.alloc((NB, 512), BF16) for _ in range(2)]
        BGW = [Buf("GW%d" % i) for i in range(2)]
        gcnt = 0
        it = 0
        ycnt = 0
        for eb in range(128 // NB):
            sl = eb % 2
            wb_, ub_, vb_, bb = wblk[sl], ublk[sl], vblk[sl], Bblk[sl]
            p.op("sp", R.dma_start(out=wb_, in_=wd[eb * NB:(eb + 1) * NB].rearrange("a p t -> p a t")),
                 reads=[B_wd], writes=[bb], dma=True)
            p.op("pool", wload(ub_, uT_d[L, :, eb * NB * 128:(eb + 1) * NB * 128]), writes=[bb], dma=True)
            p.op("pool", R.dma_start(
                out=vb_, in_=pv_d[L, eb * NB * 128:(eb + 1) * NB * 128, :].rearrange("(a p) d -> p a d", p=128)),
                writes=[bb], dma=True)
            for tg in range(4):
                gw, bgw = GWs[it % 2], BGW[it % 2]
                it += 1
                for a in range(NB):
                    sb = gcnt % 2
                    G, bG = Gs[gcnt % 3], BG[gcnt % 3]
                    gcnt += 1
                    for kc in range(8):
                        p.op("pe", R.matmul(
                            banks[sb][:, 0:512], lhsT=ub_[:, kc, a * 128:(a + 1) * 128], rhs=hnT[:, kc, tg * 512:(tg + 1) * 512],
                            start=(kc == 0), stop=(kc == 7)), reads=[bb, B_hnT], writes=[PB[sb]])
                    p.op("act", R.activation(out=G, in_=banks[sb][:, 0:512], func=AF.Gelu),
                         reads=[PB[sb]], writes=[bG])
                    p.op("dve", R.tensor_tensor(
                        out=gw[:, a, :], in0=G, in1=wb_[:, a, tg * 512:(tg + 1) * 512], op=ALU.mult),
                        reads=[bG, bb], writes=[bgw])
                for dc in range(8):
                    yb_ = 2 + (ycnt % 4)
                    ycnt += 1
                    for a in range(NB):
                        p.op("pe", R.matmul(
                            banks[yb_][:, 0:512], lhsT=vb_[:, a, dc * 128:(dc + 1) * 128], rhs=gw[:, a, :],
                            start=(a == 0), stop=(a == NB - 1)), reads=[bb, bgw], writes=[PB[yb_]])
                    ysl = yacc[:, dc, tg * 512:(tg + 1) * 512]
                    if eb == 0:
                        p.op("dve", R.tensor_copy(out=ysl, in_=banks[yb_][:, 0:512]),
                             reads=[PB[yb_]], writes=[Byacc])
                    else:
                        p.op("dve", R.tensor_tensor(out=ysl, in0=ysl, in1=banks[yb_][:, 0:512], op=ALU.add),
                             reads=[PB[yb_], Byacc], writes=[Byacc])

        p.barrier()
        hts = [A.alloc((D,)) for _ in range(2)]
        Bht = [Buf("ht%d" % i) for i in range(2)]
        for j in range(NT):
            rows = slice(j * 128, (j + 1) * 128)
            ht, bh = hts[j % 2], Bht[j % 2]
            p.op("sp", R.dma_start(out=ht, in_=hbuf[rows, :]), reads=[B_h], writes=[bh], dma=True)
            for hf in range(2):
                pb = 4 + hf
                for c in range(4):
                    dc = hf * 4 + c
                    p.op("pe", R.transpose(
                        out=banks[pb][:, c * 128:(c + 1) * 128], in_=yacc[:, dc, j * 128:(j + 1) * 128], identity=ident_f),
                        reads=[Byacc, B_const], writes=[PB[pb]])
                p.op("dve", R.tensor_tensor(
                    out=ht[:, hf * 512:(hf + 1) * 512], in0=ht[:, hf * 512:(hf + 1) * 512], in1=banks[pb][:, 0:512], op=ALU.add),
                    reads=[PB[pb], bh], writes=[bh])
            p.op("sp", R.dma_start(out=hbuf[rows, :], in_=ht), reads=[bh], writes=[B_h], dma=True)

    stages = dbg["stages"] if dbg else None

    def want(name):
        return stages is None or name in stages

    for L in range(DEPTH):
        if dbg and L >= dbg.get("layers", DEPTH):
            break
        norm_phase(mixn_d[L:L + 1, :], src=(x_d if L == 0 else None))
        if want("ssd"):
            ssd_mixer(L)
        if want("ret"):
            ret_mixer(L)
        if want("moba"):
            softmax_mixer(L, 2)
        if want("dil"):
            softmax_mixer(L, 3)
        if want("merge"):
            merge_phase(L)
        if want("cross"):
            norm_phase(xn_d[L:L + 1, :])
            cross_phase(L)
        if want("peer"):
            norm_phase(fn_d[L:L + 1, :])
            peer_phase(L)
    norm_phase(fnorm_d[0:1, :], final_out=out_d)
    fin = p.op("sp", None)
    fin.deps = list(p.all_dma[-40:])
    p.emit(st)
    return nc, st


_CACHE = {}


def _prep_weights(inp):
    w_in = np.ascontiguousarray(inp["w_in"], dtype=np.float32)
    sw = []
    for c0 in (C_RQ, C_RK, C_MQ, C_MK, C_DQ, C_DK):
        sw.append(np.stack([_swap_halves(w_in[L, :, c0:c0 + 512]) for L in range(DEPTH)], 0))
    w_sw = np.ascontiguousarray(np.concatenate(sw, axis=2))
    sk = np.asarray(inp["peer_sub_keys"], dtype=np.float32)
    skT = np.ascontiguousarray(sk.reshape(DEPTH, 16, 128, 128).transpose(0, 1, 3, 2))
    uT = np.ascontiguousarray(np.asarray(inp["peer_u"], dtype=np.float32).transpose(0, 2, 1))
    shared = {
        "w_in": w_in, "w_sw": w_sw, "skT": skT, "peer_uT": uT,
        "peer_v": np.ascontiguousarray(inp["peer_v"], dtype=np.float32),
        "final_norm": np.ascontiguousarray(inp["final_norm"], dtype=np.float32).reshape(1, D),
    }
    for k in ("ssd_conv_w", "ssd_conv_b", "ssd_dt_bias", "ssd_a_log", "ssd_d", "ssd_norm", "ret_norm", "w_branch",
              "w_out", "mix_norm", "x_norm", "w_xq", "w_xkv", "w_xo", "ffn_norm", "w_pq"):
        shared[k] = np.ascontiguousarray(inp[k], dtype=np.float32)
    for k, v in _consts().items():
        shared["c_" + k] = v
    return shared


def kernel(**inputs):
    x = np.asarray(inputs["x"], dtype=np.float32)
    mem = np.asarray(inputs["mem"], dtype=np.float32)
    nb = x.shape[0]
    shared = _prep_weights(inputs)
    if "nc" not in _CACHE:
        _CACHE["nc"] = build()
    nc, _st = _CACHE["nc"]
    in_maps = []
    for b in range(nb):
        m = dict(shared)
        m["x"] = np.ascontiguousarray(x[b])
        m["mem"] = np.ascontiguousarray(mem[b])
        in_maps.append(m)
    res = run_bass_kernel_spmd(nc, in_maps, core_ids=list(range(nb)))
    return np.stack([np.asarray(r["out"], dtype=np.float32) for r in res.results], 0)
```
